# Optimizing a Trainium2 kernel written in Bass

```python
import jax, jax.numpy as jnp
from jax import lax
import numpy as np

D_MODEL = 2048
BATCH = 4
SEQ = 2048
DEPTH = 4

CHUNK = 64
N_MIXERS = 4
D_FF = 4 * D_MODEL
NORM_EPS = 1e-6

RET_HEADS = 8
RET_DK = D_MODEL // RET_HEADS
RET_DV = 2 * D_MODEL // RET_HEADS
RET_IN = 2 * RET_HEADS * RET_DK + 2 * RET_HEADS * RET_DV
ROPE_BASE = 10000.0

GDN_HEADS = 16
GDN_DK = D_MODEL // GDN_HEADS
GDN_DV = D_MODEL // GDN_HEADS
GDN_QKV = GDN_HEADS * (2 * GDN_DK + GDN_DV)
GDN_IN = GDN_QKV + GDN_HEADS * GDN_DV + 2 * GDN_HEADS
CONV_WIDTH = 4

GLA_HEADS = 4
GLA_DK = D_MODEL // 2 // GLA_HEADS
GLA_DV = D_MODEL // GLA_HEADS
GLA_GATE_RANK = 16
GLA_TAU = 16.0
GLA_IN = 2 * GLA_HEADS * GLA_DK + 2 * GLA_HEADS * GLA_DV + GLA_GATE_RANK

LRU_WIDTH = D_MODEL
LRU_BLOCKS = 16
LRU_BLOCK = LRU_WIDTH // LRU_BLOCKS
LRU_C = 8.0

kernel_name = "interleaved_hybrid_chunk_causal_encoder"

F32 = jnp.float32


def _layers_of(m):
    return (DEPTH - m + N_MIXERS - 1) // N_MIXERS


def rmsnorm(x, g):
    xf = x.astype(F32)
    y = xf * lax.rsqrt(jnp.mean(xf * xf, axis=-1, keepdims=True) + NORM_EPS)
    return (y * g.astype(F32)).astype(x.dtype)


def head_norm(o, gain, center):
    of = o.astype(F32)
    if center:
        of = of - jnp.mean(of, axis=-1, keepdims=True)
    of = of * lax.rsqrt(jnp.mean(of * of, axis=-1, keepdims=True) + NORM_EPS)
    of = of * gain.astype(F32)
    return of.reshape(o.shape[0], o.shape[1], -1)


def l2norm(x):
    return x * lax.rsqrt(jnp.sum(x * x, axis=-1, keepdims=True) + NORM_EPS)


def causal_depthwise_conv(x, w):
    width, s = w.shape[0], x.shape[1]
    xp = jnp.pad(x, ((0, 0), (width - 1, 0), (0, 0)))
    out = xp[:, 0:s] * w[0]
    for tap in range(1, width):
        out = out + xp[:, tap:tap + s] * w[tap]
    return out


def to_chunks(x):
    b, s, h, d = x.shape
    return x.reshape(b, s // CHUNK, CHUNK, h, d).transpose(0, 3, 1, 2, 4)


def from_chunks(x):
    b, h, n, c, d = x.shape
    return x.transpose(0, 2, 3, 1, 4).reshape(b, n * c, h, d)


def scalar_chunks(x):
    b, s, h = x.shape
    return x.reshape(b, s // CHUNK, CHUNK, h).transpose(0, 3, 1, 2)


def rotary(x):
    s, d = x.shape[1], x.shape[-1]
    inv = ROPE_BASE ** (-jnp.arange(0, d, 2, dtype=F32) / d)
    ang = jnp.arange(s, dtype=F32)[:, None] * inv[None, :]
    cos, sin = jnp.cos(ang)[:, None, :], jnp.sin(ang)[:, None, :]
    x1, x2 = x[..., : d // 2], x[..., d // 2:]
    return jnp.concatenate([x1 * cos - x2 * sin, x1 * sin + x2 * cos], axis=-1)


def retention_mixer(h, w_in, gn_gain, w_out):
    b, s, _ = h.shape
    H, DK, DV = RET_HEADS, RET_DK, RET_DV
    q, k, v, g = jnp.split(h @ w_in, [H * DK, 2 * H * DK, 2 * H * DK + H * DV], axis=-1)
    q = rotary(q.astype(F32).reshape(b, s, H, DK))
    k = rotary(k.astype(F32).reshape(b, s, H, DK)) * DK ** -0.5
    v = v.astype(F32).reshape(b, s, H, DV)
    qc, kc, vc = to_chunks(q), to_chunks(k), to_chunks(v)
    log_gamma = jnp.log1p(-jnp.exp2(-5.0 - jnp.arange(H, dtype=F32)))
    pos = jnp.arange(CHUNK, dtype=F32)
    dist = jnp.abs(pos[:, None] - pos[None, :])
    decay_intra = jnp.exp(log_gamma[:, None, None] * dist)
    scores = jnp.einsum('bhncd,bhnmd->bhncm', qc, kc) * decay_intra[None, :, None]
    o_intra = jnp.einsum('bhncm,bhnmv->bhncv', scores, vc)
    q_decay = jnp.exp(log_gamma[:, None] * (pos + 1.0))[None, :, :, None]
    k_decay = jnp.exp(log_gamma[:, None] * (CHUNK - 1.0 - pos))[None, :, :, None]
    chunk_decay = jnp.exp(log_gamma * CHUNK)[None, :, None, None]

    def step(state, xs):
        q_n, k_n, v_n = xs
        o = jnp.einsum('bhcd,bhdv->bhcv', q_n * q_decay, state)
        state = state * chunk_decay + jnp.einsum('bhcd,bhcv->bhdv', k_n * k_decay, v_n)
        return state, o

    xs = (jnp.moveaxis(qc, 2, 0), jnp.moveaxis(kc, 2, 0), jnp.moveaxis(vc, 2, 0))
    _, o_inter = lax.scan(step, jnp.zeros((b, H, DK, DV), F32), xs)
    o = from_chunks(o_intra + jnp.moveaxis(o_inter, 0, 2))
    o = head_norm(o, gn_gain, center=True) * jax.nn.silu(g.astype(F32))
    return o.astype(h.dtype) @ w_out


def gated_deltanet_mixer(h, w_in, conv_w, a_log, dt_bias, norm_gain, w_out):
    b, s, _ = h.shape
    H, DK, DV = GDN_HEADS, GDN_DK, GDN_DV
    qkv, z, beta_logit, a_logit = jnp.split(
        h @ w_in, [GDN_QKV, GDN_QKV + H * DV, GDN_QKV + H * DV + H], axis=-1)
    qkv = jax.nn.silu(causal_depthwise_conv(qkv, conv_w)).astype(F32)
    q, k, v = jnp.split(qkv, [H * DK, 2 * H * DK], axis=-1)
    q = l2norm(q.reshape(b, s, H, DK)) * DK ** -0.5
    k = l2norm(k.reshape(b, s, H, DK))
    v = v.reshape(b, s, H, DV)
    beta = jax.nn.sigmoid(beta_logit.astype(F32))
    log_alpha = -jnp.exp(a_log.astype(F32)) * jax.nn.softplus(a_logit.astype(F32) + dt_bias.astype(F32))
    qc, kc, vc = to_chunks(q), to_chunks(k), to_chunks(v)
    beta_c = scalar_chunks(beta)
    cum = jnp.cumsum(scalar_chunks(log_alpha), axis=-1)
    idx = jnp.arange(CHUNK)
    strict = idx[:, None] > idx[None, :]
    rel = jnp.where(strict, jnp.exp(jnp.where(strict, cum[..., :, None] - cum[..., None, :], 0.0)), 0.0)
    a_mat = beta_c[..., :, None] * rel * jnp.einsum('bhncd,bhnmd->bhncm', kc, kc)
    l_mat = a_mat + jnp.eye(CHUNK, dtype=F32)
    rhs = jnp.concatenate([beta_c[..., None] * vc, (beta_c * jnp.exp(cum))[..., None] * kc], axis=-1)
    sol = lax.linalg.triangular_solve(l_mat, rhs, left_side=True, lower=True, unit_diagonal=True)
    u, w = sol[..., :DV], sol[..., DV:]
    k_end = kc * jnp.exp(cum[..., -1:] - cum)[..., None]
    trans = (jnp.exp(cum[..., -1])[..., None, None] * jnp.eye(DK, dtype=F32)
             - jnp.einsum('bhnck,bhncj->bhnkj', k_end, w))
    inject = jnp.einsum('bhnck,bhncv->bhnkv', k_end, u)

    def step(state, xs):
        t, g, q_n = xs
        state = jnp.einsum('bhkj,bhjv->bhkv', t, state) + g
        return state, jnp.einsum('bhck,bhkv->bhcv', q_n, state)

    xs = (jnp.moveaxis(trans, 2, 0), jnp.moveaxis(inject, 2, 0), jnp.moveaxis(qc, 2, 0))
    _, o = lax.scan(step, jnp.zeros((b, H, DK, DV), F32), xs)
    o = from_chunks(jnp.moveaxis(o, 0, 2))
    o = head_norm(o, norm_gain, center=False) * jax.nn.silu(z.astype(F32))
    return o.astype(h.dtype) @ w_out


def gla_mixer(h, w_in, w_gate_up, gate_bias, norm_gain, w_out):
    b, s, _ = h.shape
    H, DK, DV = GLA_HEADS, GLA_DK, GLA_DV
    QK, V = H * DK, H * DV
    q, k, v, r, gate_low = jnp.split(h @ w_in, [QK, 2 * QK, 2 * QK + V, 2 * QK + 2 * V], axis=-1)
    gate_logit = (gate_low @ w_gate_up + gate_bias).astype(F32)
    log_alpha = jax.nn.log_sigmoid(gate_logit) / GLA_TAU
    qc = to_chunks(q.astype(F32).reshape(b, s, H, DK)) * DK ** -0.5
    kc = to_chunks(k.astype(F32).reshape(b, s, H, DK))
    vc = to_chunks(v.astype(F32).reshape(b, s, H, DV))
    cum = jnp.cumsum(to_chunks(log_alpha.reshape(b, s, H, DK)), axis=-2)
    ref = cum[..., CHUNK // 2 - 1:CHUNK // 2, :]
    fwd, bwd = jnp.exp(cum - ref), jnp.exp(ref - cum)
    s_lo = jnp.einsum('bhnck,bhnmk->bhncm', qc * fwd, kc * bwd)
    s_up = jnp.einsum('bhnck,bhnmk->bhncm', qc * bwd, kc * fwd)
    idx = jnp.arange(CHUNK)
    scores = jnp.where(idx[:, None] >= idx[None, :], s_lo, s_up)
    o_intra = jnp.einsum('bhncm,bhnmv->bhncv', scores, vc)
    q_in = qc * jnp.exp(cum)
    k_end = kc * jnp.exp(cum[..., -1:, :] - cum)
    chunk_dec = jnp.exp(cum[..., -1, :])

    def step(state, xs):
        q_n, k_n, v_n, dec = xs
        o = jnp.einsum('bhck,bhkv->bhcv', q_n, state)
        state = state * dec[..., None] + jnp.einsum('bhck,bhcv->bhkv', k_n, v_n)
        return state, o

    xs = tuple(jnp.moveaxis(t, 2, 0) for t in (q_in, k_end, vc, chunk_dec))
    _, o_inter = lax.scan(step, jnp.zeros((b, H, DK, DV), F32), xs)
    o = from_chunks(o_intra + jnp.moveaxis(o_inter, 0, 2))
    o = head_norm(o, norm_gain, center=False) * jax.nn.silu(r.astype(F32))
    return o.astype(h.dtype) @ w_out


def rglru_mixer(h, w_in, conv_w, conv_b, w_rgate, b_rgate, w_igate, b_igate, lam, w_out):
    b, s, _ = h.shape
    xb, yb = jnp.split(h @ w_in, [LRU_WIDTH], axis=-1)
    yb = jax.nn.gelu(yb.astype(F32))
    xb = (causal_depthwise_conv(xb, conv_w) + conv_b).astype(F32)
    xblk = xb.reshape(b, s, LRU_BLOCKS, LRU_BLOCK)
    r = jax.nn.sigmoid(jnp.einsum('bsnd,nde->bsne', xblk, w_rgate.astype(F32)) + b_rgate.astype(F32))
    i = jax.nn.sigmoid(jnp.einsum('bsnd,nde->bsne', xblk, w_igate.astype(F32)) + b_igate.astype(F32))
    r, i = r.reshape(b, s, LRU_WIDTH), i.reshape(b, s, LRU_WIDTH)
    log_a = -LRU_C * jax.nn.softplus(-lam.astype(F32)) * r
    a = jnp.exp(log_a)
    u = jnp.sqrt(-jnp.expm1(2.0 * log_a)) * (i * xb)

    def combine(left, right):
        a_l, b_l = left
        a_r, b_r = right
        return a_l * a_r, a_r * b_l + b_r

    _, hs = lax.associative_scan(combine, (a, u), axis=1)
    return (hs * yb).astype(h.dtype) @ w_out


def squared_relu_mlp(h, w_up, w_down):
    return jnp.square(jax.nn.relu(h @ w_up)) @ w_down


def setup_inputs(seed: int = 0) -> dict:
    key = jax.random.key(seed)
    ks = iter(jax.random.split(key, 40))

    def nrm(shape, scale):
        return scale * jax.random.normal(next(ks), shape, F32)

    def uni(shape, lo, hi):
        return jax.random.uniform(next(ks), shape, F32, lo, hi)

    n_ret, n_gdn, n_gla, n_lru = (_layers_of(m) for m in range(N_MIXERS))
    d_in = D_MODEL ** -0.5
    dt = jnp.exp(uni((n_gdn, GDN_HEADS), float(np.log(1e-3)), float(np.log(1e-1))))
    a0 = uni((n_lru, LRU_WIDTH), 0.9, 0.999)
    s0 = a0 ** (1.0 / LRU_C)
    return {
        "x": nrm((BATCH, SEQ, D_MODEL), 1.0),
        "norm1": 1.0 + nrm((DEPTH, D_MODEL), 0.02),
        "norm2": 1.0 + nrm((DEPTH, D_MODEL), 0.02),
        "final_norm": 1.0 + nrm((D_MODEL,), 0.02),
        "ret_w_in": nrm((n_ret, D_MODEL, RET_IN), d_in),
        "ret_gn_gain": 1.0 + nrm((n_ret, RET_HEADS, RET_DV), 0.02),
        "ret_w_out": nrm((n_ret, RET_HEADS * RET_DV, D_MODEL), (RET_HEADS * RET_DV) ** -0.5),
        "gdn_w_in": nrm((n_gdn, D_MODEL, GDN_IN), d_in),
        "gdn_conv_w": nrm((n_gdn, CONV_WIDTH, GDN_QKV), CONV_WIDTH ** -0.5),
        "gdn_a_log": jnp.log(uni((n_gdn, GDN_HEADS), 1.0, 16.0)),
        "gdn_dt_bias": dt + jnp.log(-jnp.expm1(-dt)),
        "gdn_norm_gain": 1.0 + nrm((n_gdn, GDN_DV), 0.02),
        "gdn_w_out": nrm((n_gdn, GDN_HEADS * GDN_DV, D_MODEL), (GDN_HEADS * GDN_DV) ** -0.5),
        "gla_w_in": nrm((n_gla, D_MODEL, GLA_IN), d_in),
        "gla_w_gate_up": nrm((n_gla, GLA_GATE_RANK, GLA_HEADS * GLA_DK), GLA_GATE_RANK ** -0.5),
        "gla_gate_bias": nrm((n_gla, GLA_HEADS * GLA_DK), 0.1),
        "gla_norm_gain": 1.0 + nrm((n_gla, GLA_HEADS, GLA_DV), 0.02),
        "gla_w_out": nrm((n_gla, GLA_HEADS * GLA_DV, D_MODEL), (GLA_HEADS * GLA_DV) ** -0.5),
        "lru_w_in": nrm((n_lru, D_MODEL, 2 * LRU_WIDTH), d_in),
        "lru_conv_w": nrm((n_lru, CONV_WIDTH, LRU_WIDTH), CONV_WIDTH ** -0.5),
        "lru_conv_b": nrm((n_lru, LRU_WIDTH), 0.02),
        "lru_w_rgate": nrm((n_lru, LRU_BLOCKS, LRU_BLOCK, LRU_BLOCK), LRU_BLOCK ** -0.5),
        "lru_b_rgate": nrm((n_lru, LRU_BLOCKS, LRU_BLOCK), 0.02),
        "lru_w_igate": nrm((n_lru, LRU_BLOCKS, LRU_BLOCK, LRU_BLOCK), LRU_BLOCK ** -0.5),
        "lru_b_igate": nrm((n_lru, LRU_BLOCKS, LRU_BLOCK), 0.02),
        "lru_lambda": jnp.log(s0) - jnp.log1p(-s0),
        "lru_w_out": nrm((n_lru, LRU_WIDTH, D_MODEL), LRU_WIDTH ** -0.5),
        "mlp_w_up": nrm((DEPTH, D_MODEL, D_FF), d_in),
        "mlp_w_down": nrm((DEPTH, D_FF, D_MODEL), D_FF ** -0.5),
    }


def reference(x, norm1, norm2, final_norm,
              ret_w_in, ret_gn_gain, ret_w_out,
              gdn_w_in, gdn_conv_w, gdn_a_log, gdn_dt_bias, gdn_norm_gain, gdn_w_out,
              gla_w_in, gla_w_gate_up, gla_gate_bias, gla_norm_gain, gla_w_out,
              lru_w_in, lru_conv_w, lru_conv_b, lru_w_rgate, lru_b_rgate, lru_w_igate,
              lru_b_igate, lru_lambda, lru_w_out,
              mlp_w_up, mlp_w_down):
    for layer in range(DEPTH):
        m, j = layer % N_MIXERS, layer // N_MIXERS
        hn = rmsnorm(x, norm1[layer])
        if m == 0:
            y = retention_mixer(hn, ret_w_in[j], ret_gn_gain[j], ret_w_out[j])
        elif m == 1:
            y = gated_deltanet_mixer(hn, gdn_w_in[j], gdn_conv_w[j], gdn_a_log[j], gdn_dt_bias[j],
                                     gdn_norm_gain[j], gdn_w_out[j])
        elif m == 2:
            y = gla_mixer(hn, gla_w_in[j], gla_w_gate_up[j], gla_gate_bias[j], gla_norm_gain[j], gla_w_out[j])
        else:
            y = rglru_mixer(hn, lru_w_in[j], lru_conv_w[j], lru_conv_b[j], lru_w_rgate[j], lru_b_rgate[j],
                            lru_w_igate[j], lru_b_igate[j], lru_lambda[j], lru_w_out[j])
        x = x + y
        x = x + squared_relu_mlp(rmsnorm(x, norm2[layer]), mlp_w_up[layer], mlp_w_down[layer])
    return rmsnorm(x, final_norm)
```

```python
import numpy as np
import ml_dtypes
import concourse.bass as bass
import concourse.mybir as mybir
from concourse.bass_utils import run_bass_kernel_spmd

F32 = mybir.dt.float32
BF16 = mybir.dt.bfloat16
AF = mybir.ActivationFunctionType
ALU = mybir.AluOpType
AX = mybir.AxisListType

D = 2048
DC = D // 128
SEQ = 2048
TT = 512
NH = TT // 512
DFF = 8192
EPS = 1e-6
ERA = 30000
RING = 8
STREAMS = ("pe", "act", "dve", "pool", "sp")


class Buf:
    __slots__ = ("name", "w", "r", "psum", "keep")

    def __init__(self, name, psum=False):
        self.name = name
        self.w = None
        self.r = {}
        self.psum = psum
        self.keep = False


class V:
    __slots__ = ("ap", "bufs")

    def __init__(self, ap, bufs):
        self.ap = ap
        self.bufs = bufs

    def __getitem__(self, idx):
        return V(self.ap[idx], self.bufs)

    def re(self, s, **kw):
        return V(self.ap.rearrange(s, **kw), self.bufs)

    def bc(self, shape):
        return V(self.ap.to_broadcast(shape), self.bufs)

    def w(self, ap):
        return V(ap, self.bufs)


def _ap(x):
    return x.ap if isinstance(x, V) else x


class _Dummy:
    def __getitem__(self, idx):
        return self

    def rearrange(self, *a, **k):
        return self

    def to_broadcast(self, *a, **k):
        return self

    def bitcast(self, *a, **k):
        return self


DUMMY = _Dummy()


class Prog:
    def __init__(self, nc, dry=False):
        self.nc = nc
        self.dry = dry
        self.q = {s: [] for s in STREAMS}
        self.pend = {s: [] for s in STREAMS}
        self.seen_c = {s: {} for s in STREAMS}
        self.seen_d = {s: {} for s in STREAMS}
        self.ndma = {s: 0 for s in STREAMS}
        self.bufs = []
        self.dma_tokens = []
        self.bg_tokens = []
        self.arena_base = None
        self.sb_off = 0
        self.sb_limit = 0
        self.ntile = 0
        self.psum_stack = None

    def init_arena(self, nbytes):
        self.sb_off = 0
        self.sb_limit = nbytes
        if self.dry:
            return
        a = self.nc.alloc_sbuf_tensor("arena", [128, nbytes], mybir.dt.uint8)
        self.arena_base = self.nc.lookup_mloc(a).addr
        self.sb_off = 0
        self.sb_limit = nbytes

    def tile(self, name, shape, dtype, off=None):
        esz = 2 if dtype == BF16 else 4
        n = 1
        for s in shape[1:]:
            n *= s
        nbytes = ((n * esz + 63) // 64) * 64
        if off is None:
            off = self.sb_off
            self.sb_off += nbytes
        assert off + nbytes <= self.sb_limit, (name, off, nbytes, self.sb_limit)
        self.ntile += 1
        if self.dry:
            return V(DUMMY, [])
        t = self.nc.alloc_sbuf_tensor_at(f"{name}_{self.ntile}", list(shape), dtype,
                                         offset=self.arena_base + off)
        b = Buf(name)
        self.bufs.append(b)
        return V(t.ap(), [b])

    def psum(self, name, shape, dtype=F32):
        if self.dry:
            return V(DUMMY, [])
        t = self.nc.alloc_psum_tensor(name, list(shape), dtype)
        b = Buf(name, psum=True)
        self.bufs.append(b)
        return V(t.ap(), [b])

    def dram_view(self, ap, name="dram"):
        b = Buf(name)
        self.bufs.append(b)
        return V(ap, [b])

    def _need(self, stream, tok, same_ok):
        if tok is None:
            return
        if tok[0] == "c":
            _, e, idx = tok
            if e == stream and (same_ok or stream == "pe"):
                return
            if self.seen_c[stream].get(e, -1) >= idx:
                return
            self.seen_c[stream][e] = idx
            self.q[e][idx]["signal"] = True
            self.pend[stream].append(tok)
        else:
            _, sem, val = tok
            if self.seen_d[stream].get(sem, -1) >= val:
                return
            self.seen_d[stream][sem] = val
            self.pend[stream].append(tok)

    def I(self, stream, fn, reads=(), writes=(), dma=False, bg=False):
        if self.dry:
            return None
        rb = []
        for v in reads:
            if isinstance(v, V):
                rb.extend(v.bufs)
        wb = []
        for v in writes:
            if isinstance(v, V):
                wb.extend(v.bufs)
        for b in rb:
            self._need(stream, b.w, False)
            if b.psum:
                for t in b.r.values():
                    self._need(stream, t, True)
        for b in wb:
            self._need(stream, b.w, False)
            for t in b.r.values():
                self._need(stream, t, False)
        idx = len(self.q[stream])
        rec = {"fn": fn, "waits": self.pend[stream], "signal": False, "dma": None}
        self.pend[stream] = []
        if dma:
            n = self.ndma[stream]
            self.ndma[stream] += 1
            slot = n % RING
            val = 16 * (n // RING + 1)
            if n >= RING:
                t = ("d", (stream, slot), val - 16)
                if self.seen_d[stream].get(t[1], -1) < t[2]:
                    self.seen_d[stream][t[1]] = t[2]
                    rec["waits"].append(t)
            rec["dma"] = ((stream, slot), val)
            tok = ("d", (stream, slot), val)
            if not bg:
                self.dma_tokens.append(tok)
            else:
                self.bg_tokens.append(tok)
        else:
            tok = ("c", stream, idx)
        self.q[stream].append(rec)
        for b in rb:
            b.r[tok[:2]] = tok
        for b in wb:
            b.w = tok
            b.r = {}
        return tok

    def barrier(self, final=False):
        if self.dry:
            return
        if final:
            self.dma_tokens += self.bg_tokens
            self.bg_tokens = []
        toks = []
        for s in STREAMS:
            for idx in range(len(self.q[s]) - 1, -1, -1):
                if self.q[s][idx]["dma"] is None:
                    toks.append(("c", s, idx))
                    break
        last = {}
        for t in self.dma_tokens:
            last[t[1]] = max(last.get(t[1], 0), t[2])
        for sem, val in last.items():
            toks.append(("d", sem, val))
        for s in STREAMS:
            for t in toks:
                self._need(s, t, False)
        self.dma_tokens = []
        for b in self.bufs:
            if b.keep and not final:
                continue
            b.w = None
            b.r = {}

    def mm(self, out, lhsT, rhs, start=True, stop=True):
        o, l, r = _ap(out), _ap(lhsT), _ap(rhs)
        return self.I("pe", lambda e: e.matmul(o, l, r, start=start, stop=stop),
                      reads=[lhsT, rhs], writes=[out])

    def tr(self, out, in_, ident):
        o, i, d = _ap(out), _ap(in_), _ap(ident)
        return self.I("pe", lambda e: e.transpose(o, i, d), reads=[in_, ident], writes=[out])

    def act(self, out, in_, func, bias=0.0, scale=1.0, accum_out=None, eng="act"):
        o, i, b, s = _ap(out), _ap(in_), _ap(bias), _ap(scale)
        kw = {}
        if accum_out is not None:
            kw["accum_out"] = _ap(accum_out)
        return self.I(eng, lambda e: e.activation(o, i, func, bias=b, scale=s, **kw),
                      reads=[in_, bias, scale], writes=[out] + ([accum_out] if accum_out is not None else []))

    def tt(self, eng, out, a, b, op):
        o, x, y = _ap(out), _ap(a), _ap(b)
        return self.I(eng, lambda e: e.tensor_tensor(o, x, y, op), reads=[a, b], writes=[out])

    def ts(self, eng, out, a, s1, s2, op0, op1=None, accum_out=None):
        o, x, p, q = _ap(out), _ap(a), _ap(s1), _ap(s2)
        kw = {}
        if op1 is not None:
            kw["op1"] = op1
        if accum_out is not None:
            kw["accum_out"] = _ap(accum_out)
        return self.I(eng, lambda e: e.tensor_scalar(o, x, p, q, op0, **kw),
                      reads=[a, s1, s2], writes=[out] + ([accum_out] if accum_out is not None else []))

    def stt(self, eng, out, a, scalar, b, op0, op1):
        o, x, s, y = _ap(out), _ap(a), _ap(scalar), _ap(b)
        return self.I(eng, lambda e: e.scalar_tensor_tensor(o, x, s, y, op0, op1),
                      reads=[a, scalar, b], writes=[out])

    def copy(self, eng, out, in_):
        o, i = _ap(out), _ap(in_)
        if eng == "act":
            return self.I(eng, lambda e: e.copy(o, i), reads=[in_], writes=[out])
        return self.I(eng, lambda e: e.tensor_copy(o, i), reads=[in_], writes=[out])

    def recip(self, out, in_):
        o, i = _ap(out), _ap(in_)
        return self.I("dve", lambda e: e.reciprocal(o, i), reads=[in_], writes=[out])

    def rsqrt(self, out, in_, eps, scale=1.0):
        self.act(out, in_, AF.Sqrt, bias=eps, scale=scale)
        self.recip(out, out)

    def memset(self, eng, out, val):
        o = _ap(out)
        return self.I(eng, lambda e: e.memset(o, val), writes=[out])

    def dma(self, stream, out, in_, bg=False):
        o, i = _ap(out), _ap(in_)
        return self.I(stream, lambda e: e.dma_start(out=o, in_=i), reads=[in_], writes=[out], dma=True, bg=bg)

    def emit(self, final_tokens):
        nc = self.nc
        for t in final_tokens:
            self._need("sp", t, False)
        self.barrier(final=True)
        cum = {}
        nsig = {}
        for s in STREAMS:
            c = 0
            arr = []
            for rec in self.q[s]:
                if rec["signal"]:
                    c += 1
                arr.append(c)
            cum[s] = arr
            nsig[s] = c
        import contextlib
        with contextlib.ExitStack() as es:
            csem = {}
            for s in STREAMS:
                n_era = (nsig[s] + ERA - 1) // ERA
                csem[s] = [es.enter_context(nc.semaphore(f"c_{s}_{k}")) for k in range(max(n_era, 1))]
            dsem = {}
            for s in STREAMS:
                if self.ndma[s]:
                    for k in range(RING):
                        dsem[(s, k)] = es.enter_context(nc.semaphore(f"d_{s}_{k}"))
            block = es.enter_context(nc.Block())
            engs = {"pe": block.tensor, "act": block.scalar, "dve": block.vector,
                    "pool": block.gpsimd, "sp": block.sync}

            def run_stream(s):
                def body(e):
                    def do_waits(ws):
                        for t in ws:
                            if t[0] == "c":
                                _, te, idx = t
                                c = cum[te][idx]
                                era = (c - 1) // ERA
                                e.wait_ge(csem[te][era], c - era * ERA)
                            else:
                                e.wait_ge(dsem[t[1]], t[2])
                    for i, rec in enumerate(self.q[s]):
                        do_waits(rec["waits"])
                        ins = rec["fn"](e)
                        if rec["dma"] is not None:
                            ins.then_inc(dsem[rec["dma"][0]], 16)
                        elif rec["signal"]:
                            c = cum[s][i]
                            ins.then_inc(csem[s][(c - 1) // ERA], 1)
                    do_waits(self.pend[s])
                return body

            for s in STREAMS:
                engs[s](run_stream(s))


WSLOT = 8192
NWS = 4


class WQ:
    def __init__(self, P, plan, resolve):
        self.P = P
        self.plan = plan
        self.resolve = resolve
        self.keys = []
        self.cache = {}
        self.scratch = None
        self.nxt = 0
        self.issued = 0
        self.slots = [P.tile(f"wslot{i}", [128, WSLOT], BF16) for i in range(NWS)]
        for sl in self.slots:
            for b in sl.bufs:
                b.keep = True

    def _issue(self, j):
        key = self.plan[j]
        ap, nk, ncol = self.resolve(key)
        n = nk * ncol
        slot = self.slots[j % NWS]
        if key in self.cache:
            self.P.dma("sp", slot[:, 0:n], self.cache[key], bg=True)
            return
        dst = slot[:, 0:n].re("p (k n) -> p k n", k=nk)
        self.P.dma("pool", dst, ap, bg=True)
        if self.scratch is not None:
            ci = len(self.cache)
            cv = self.P.dram_view(self.scratch[ci // 100][ci % 100, :, 0:n], "wcache")
            cv.bufs[0].keep = True
            self.cache[key] = cv
            self.P.dma("sp", cv, slot[:, 0:n], bg=True)

    def get(self, key, nk, ncol):
        i = self.nxt
        self.nxt += 1
        if self.P.dry:
            self.keys.append(key)
            return V(DUMMY, [])
        assert self.plan[i] == key, (i, key, self.plan[i])
        while self.issued < min(i + NWS - 1, len(self.plan)):
            self._issue(self.issued)
            self.issued += 1
        slot = self.slots[i % NWS]
        return slot[:, 0:nk * ncol].re("p (k n) -> p k n", k=nk)


class Net:
    def __init__(self, P, dram, plan, cfg):
        self.P = P
        self.dram = dram
        self.cfg = cfg
        P.init_arena(204 * 1024)
        self.xres = [[P.tile(f"x{dc}_{hf}", [128, 512], F32) for hf in range(NH)] for dc in range(DC)]
        self.ident = P.tile("ident", [128, 128], F32)
        self.identb = P.tile("identb", [128, 128], BF16)
        self.onesD = P.tile("onesD", [128, 128], BF16)
        self.pv = P.tile("pv", [128, cfg["npv"]], F32)
        self.sq = [P.tile(f"sq{i}", [128, 512], BF16) for i in range(2)]
        self.rstd = [P.tile(f"rstd{i}", [128, 512], F32) for i in range(1)]
        self.lru_h = P.tile("lru_h", [128, 16], F32)
        self.lru_hist = P.tile("lru_hist", [128, 16, 3], F32)
        self.gdn_hist = P.tile("gdn_hist", [128, 48, 3], F32)
        self.hn_off = P.sb_off
        self.hn = [P.tile(f"hn{dc}", [128, TT], BF16) for dc in range(DC)]
        self.wq = WQ(P, plan, self.resolve)
        self.scratch_base = P.sb_off
        self.ps = [P.psum(f"ps{i}", [128, 512], F32) for i in range(7)]
        self.psb = P.psum("psb", [128, 1024], BF16)
        self.ips = 0
        self.nrot = 7
        self.tmpi = 0
        self._stv = {}

    def next_ps(self):
        p = self.ps[self.ips % self.nrot]
        self.ips += 1
        return p

    def resolve(self, key):
        name, layer, kind, a, b = key
        t = self.dram[name]
        if kind == "cols":
            ap = t[layer][:, a:a + b].rearrange("(k p) n -> p k n", p=128)
            return ap, ap.shape[1], b
        if kind == "rows":
            ap = t[layer][a:a + b, :].rearrange("(k p) n -> p k n", p=128)
            return ap, ap.shape[1], ap.shape[2]
        raise ValueError(kind)

    def pcol(self, name, layer=0, n=1, off=0):
        c = self.cfg["pvoff"][name] + layer * self.cfg["pvstride"].get(name, 0) + off
        return self.pv[:, c:c + n]

    def load_consts(self):
        P = self.P
        d = self.dram
        P.dma("sp", self.ident, d["cst"][:, 0:128])
        onesf = P.tile("onesf", [128, 128], F32, off=self.scratch_base + 40 * 1024)
        P.dma("sp", onesf, d["cst"][:, 128:256])
        P.copy("dve", self.onesD, onesf)
        P.copy("dve", self.identb, self.ident)
        P.dma("sp", self.pv, d["pv"][:, :])
        P.barrier()

    def load_x(self, pas):
        P = self.P
        d = self.dram
        P.barrier()
        P.sb_off = self.scratch_base
        xin = [P.tile(f"xin{i}", [128, D], F32) for i in range(2)]
        for tt in range(TT // 128):
            xt = xin[tt % 2]
            r0 = pas * TT + tt * 128
            P.dma("sp", xt, d["x"][r0:r0 + 128, :])
            for g in range(DC // 4):
                ps = self.next_ps()
                for j in range(4):
                    dc = g * 4 + j
                    P.tr(ps[:, j * 128:(j + 1) * 128], xt[:, dc * 128:(dc + 1) * 128], self.ident)
                for j in range(4):
                    dc = g * 4 + j
                    hf, o = divmod(tt * 128, 512)
                    P.copy("dve" if g % 2 else "act", self.xres[dc][hf][:, o:o + 128], ps[:, j * 128:(j + 1) * 128])

    def mixer(self, layer, pas):
        P = self.P
        kind = layer % 4
        if kind == 0:
            self.mixer_ret(layer, pas)
        elif kind == 1:
            self.mixer_gdn(layer, pas)
        elif kind == 2:
            self.mixer_gla(layer, pas)
        else:
            self.mixer_lru(layer, pas)

    def rmsnorm(self, gname, layer, out_tiles, out_f32=False):
        P = self.P
        sq, rstd = self.sq, self.rstd
        for hf in range(NH):
            ps = self.next_ps()
            for dc in range(DC):
                s = sq[dc % 2]
                P.act(s, self.xres[dc][hf], AF.Square)
                P.mm(ps, self.onesD, s, start=(dc == 0), stop=(dc == DC - 1))
            r = rstd[0]
            P.rsqrt(r, ps, EPS)
            for dc in range(DC):
                g = self.pcol(gname, layer, 1, dc)
                P.stt("dve", out_tiles[dc][:, hf * 512:(hf + 1) * 512], self.xres[dc][hf], g, r,
                      ALU.mult, ALU.mult)

    def mlp(self, layer):
        P = self.P
        P.barrier()
        self.rmsnorm("norm2", layer, self.hn)
        P.sb_off = self.scratch_base
        hT = [P.tile(f"hT{i}", [128, 4, TT], BF16) for i in range(2)]
        rl = [P.tile(f"rl{i}", [128, 512], F32) for i in range(3)]
        irl = 0
        for fg in range(DFF // 512):
            wup = self.wq.get(("mlp_w_up", layer, "cols", fg * 512, 512), DC, 512)
            wdn = self.wq.get(("mlp_w_down", layer, "rows", fg * 512, 512), 4, D)
            h = hT[fg % 2]
            for fs in range(4):
                for hf in range(NH):
                    ps = self.next_ps()
                    for dc in range(DC):
                        P.mm(ps, wup[:, dc, fs * 128:(fs + 1) * 128], self.hn[dc][:, hf * 512:(hf + 1) * 512],
                             start=(dc == 0), stop=(dc == DC - 1))
                    t = rl[irl % 3]
                    irl += 1
                    P.act(t, ps, AF.Relu)
                    P.tt("dve", h[:, fs, hf * 512:(hf + 1) * 512], t, t, ALU.mult)
            for dc in range(DC):
                for hf in range(NH):
                    ps = self.next_ps()
                    for k in range(4):
                        P.mm(ps, wdn[:, k, dc * 128:(dc + 1) * 128], h[:, k, hf * 512:(hf + 1) * 512],
                             start=(k == 0), stop=(k == 3))
                    P.tt("dve", self.xres[dc][hf], self.xres[dc][hf], ps, ALU.add)

    def finish(self, pas):
        P = self.P
        P.barrier()
        P.sb_off = self.scratch_base
        rs = [P.tile(f"frs{i}", [128, 512], F32) for i in range(NH)]
        tmp = [P.tile(f"ftmp{i}", [128, 4, 128], F32) for i in range(3)]
        ot = [P.tile(f"ot{i}", [128, D], F32) for i in range(2)]
        for hf in range(NH):
            ps = self.next_ps()
            for dc in range(DC):
                sq = self.sq[dc % 2]
                P.act(sq, self.xres[dc][hf], AF.Square)
                P.mm(ps, self.onesD, sq, start=(dc == 0), stop=(dc == DC - 1))
            P.rsqrt(rs[hf], ps, EPS)
        toks = []
        it = 0
        for tt in range(TT // 128):
            hf, o = divmod(tt * 128, 512)
            out = ot[tt % 2]
            for g in range(DC // 4):
                t = tmp[it % 3]
                it += 1
                for j in range(4):
                    dc = g * 4 + j
                    P.stt("dve", t[:, j, :], self.xres[dc][hf][:, o:o + 128], self.pcol("final_norm", 0, 1, dc),
                          rs[hf][:, o:o + 128], ALU.mult, ALU.mult)
                ps = self.next_ps()
                for j in range(4):
                    P.tr(ps[:, j * 128:(j + 1) * 128], t[:, j, :], self.ident)
                P.copy("act", out[:, g * 512:(g + 1) * 512], ps)
            r0 = pas * TT + tt * 128
            toks.append(P.dma("sp", self.dram["y"][r0:r0 + 128, :], out))
        return toks


WEIGHT_SHAPES = {
    "mlp_w_up": [4, D, DFF], "mlp_w_down": [4, DFF, D],
    "ret_w_in": [1, D, 12288], "ret_w_out": [1, 4096, D],
    "gdn_w_in": [1, D, 8224], "gdn_w_out": [1, D, D],
    "gla_w_in": [1, D, 6160], "gla_w_out": [1, D, D],
    "lru_w_in": [1, D, 4096], "lru_w_out": [1, D, D],
    "lru_w_rgate": [1, 16, 128, 128], "lru_w_igate": [1, 16, 128, 128],
    "rope": [128, 2 * SEQ], "ret_gn_gain": [1, 8, 512], "gla_norm_gain": [1, 4, 512],
    "wgu": [64, 1024], "gdn_a_log": [1, 16], "gdn_dt_bias": [1, 16],
}


def emit_net(net, layers, npass):
    net.load_consts()
    toks = []
    for pas in range(npass):
        net.load_x(pas)
        for (layer, do_mixer, do_mlp) in layers:
            if do_mixer:
                net.mixer(layer, pas)
            if do_mlp:
                net.mlp(layer)
        toks += net.finish(pas)
    return toks


def build_program(cfg, layers, weights_used, npass=2):
    Pd = Prog(None, dry=True)
    nd = Net(Pd, _DummyDict(), None, cfg)
    emit_net(nd, layers, npass)
    plan = nd.wq.keys
    nc = bass.Bass("TRN2", target_bir_lowering=False)
    dram = {}
    dram["x"] = nc.dram_tensor("x", [npass * TT, D], F32, kind="ExternalInput").ap()
    dram["cst"] = nc.dram_tensor("cst", [128, cfg["ncst"]], F32, kind="ExternalInput").ap()
    dram["pv"] = nc.dram_tensor("pv", [128, cfg["npv"]], F32, kind="ExternalInput").ap()
    for name in weights_used:
        dram[name] = nc.dram_tensor(name, WEIGHT_SHAPES[name], F32, kind="ExternalInput").ap()
    LAST_DRAM_NAMES.clear()
    LAST_DRAM_NAMES.update(list(weights_used) + ["x", "cst", "pv"])
    dram["y"] = nc.dram_tensor("y", [npass * TT, D], F32, kind="ExternalOutput").ap()
    for name, shape in (("st_ret", [8, 128, 1024]), ("st_gla", [4, 128, 1024]), ("st_gdn", [16, 128, 128])):
        dram[name] = nc.dram_tensor(name, shape, F32, kind="Internal").ap()
    P = Prog(nc)
    net = Net(P, dram, plan, cfg)
    nuniq = len(set(plan))
    if npass > 1 and nuniq:
        net.wq.scratch = [nc.dram_tensor(f"wcache{i}", [min(100, nuniq - i * 100), 128, WSLOT], BF16, kind="Internal").ap()
                          for i in range((nuniq + 99) // 100)]
    toks = emit_net(net, layers, npass)
    P.emit(toks)
    return nc, P


class _DummyDict:
    def __getitem__(self, k):
        return DUMMY


def fvec(v):
    v = np.asarray(v, dtype=np.float32)
    return np.ascontiguousarray(v.reshape(-1, 128).T)


def pack_params(inp):
    cols = []
    off = {}
    stride = {}
    pos = 0

    def add(name, arr2d, per_layer=0):
        nonlocal pos
        off[name] = pos
        stride[name] = per_layer
        cols.append(arr2d.astype(np.float32))
        pos += arr2d.shape[1]

    add("norm1", np.concatenate([fvec(inp["norm1"][l]) for l in range(4)], axis=1), 16)
    add("norm2", np.concatenate([fvec(inp["norm2"][l]) for l in range(4)], axis=1), 16)
    add("final_norm", fvec(inp["final_norm"]))
    if "gdn_conv_w" in inp:
        add("gdn_conv_w", np.concatenate([fvec(inp["gdn_conv_w"][0, t]) for t in range(4)], axis=1))
        add("gdn_norm_gain", fvec(inp["gdn_norm_gain"][0]))
    if "lru_lambda" in inp:
        add("lru_lambda", fvec(inp["lru_lambda"][0]))
        add("lru_conv_w", np.concatenate([fvec(inp["lru_conv_w"][0, t]) for t in range(4)], axis=1))
        add("lru_conv_b", fvec(inp["lru_conv_b"][0]))
        add("lru_b_rgate", fvec(inp["lru_b_rgate"][0].reshape(-1)))
        add("lru_b_igate", fvec(inp["lru_b_igate"][0].reshape(-1)))
    pv = np.ascontiguousarray(np.concatenate(cols, axis=1))
    return pv, off, stride


RET_H = 8
RET_DK = 256


def make_consts():
    cols = []
    idx = {}
    pos = 0

    def add(name, a):
        nonlocal pos
        idx[name] = pos
        cols.append(a.astype(np.float32))
        pos += a.shape[1]

    add("ident", np.eye(128, dtype=np.float32))
    add("ones", np.full((128, 128), 1.0 / D, np.float32))
    m = np.arange(128)[:, None]
    c = np.arange(128)[None, :]
    same = (m // 64) == (c // 64)
    add("cst_mlo", (m <= c).astype(np.float32))
    add("cst_mup", ((m > c) & same).astype(np.float32))
    lg = np.log1p(-np.exp2(-5.0 - np.arange(RET_H, dtype=np.float64)))
    t = np.arange(128, dtype=np.float64)
    tab = np.zeros((RET_H, 4, 128))
    for h in range(RET_H):
        f = np.exp(lg[h] * (t - 63.0))
        b = np.exp(lg[h] * (63.0 - t))
        tab[h, 0] = f * RET_DK ** -0.5
        tab[h, 1] = b * RET_DK ** -0.5
        tab[h, 2] = f
        tab[h, 3] = b
    add("cst_rtab", np.broadcast_to(tab.reshape(1, -1), (128, RET_H * 4 * 128)))
    rE = np.stack([np.exp(lg * 64.0), np.exp(lg * 128.0), np.exp(lg * 64.0)], axis=1)
    add("cst_rE", np.broadcast_to(rE.reshape(1, -1), (128, RET_H * 3)))
    ltx = np.zeros((128, 132), np.float32)
    ltx[:, 0:128] = (m <= c).astype(np.float32) - (m <= 63).astype(np.float32)
    ltx[:, 128] = (np.arange(128) <= 63)
    ltx[:, 129] = 1.0
    ltx[:, 130] = (np.arange(128) > 63)
    add("cst_ltx", ltx)
    add("cst_cumT", ((m <= c) & same).astype(np.float32))
    add("cst_selA", np.broadcast_to((np.arange(128) < 64).astype(np.float32)[:, None], (128, 128)))
    add("cst_selB", np.broadcast_to((np.arange(128) >= 64).astype(np.float32)[:, None], (128, 128)))
    add("cst_strict", ((c > m) & same).astype(np.float32).T.copy())
    add("cst_blk", same.astype(np.float32))
    return np.ascontiguousarray(np.concatenate(cols, axis=1)), idx


def make_rope():
    inv = 10000.0 ** (-np.arange(0, RET_DK, 2, dtype=np.float32) / RET_DK)
    ang = (np.arange(SEQ, dtype=np.float32)[None, :] * inv[:, None]).astype(np.float32)
    return np.ascontiguousarray(np.concatenate([np.cos(ang), np.sin(ang)], axis=1).astype(np.float32))


LAST_DRAM_NAMES = set()


def extra_inputs(inp):
    ex = {"rope": make_rope()}
    if "gla_w_gate_up" in inp:
        w = np.zeros((64, 1024), np.float32)
        w[0:16] = inp["gla_w_gate_up"][0]
        w[32] = inp["gla_gate_bias"][0]
        ex["wgu"] = w
    for k in ("ret_gn_gain", "gla_norm_gain", "gdn_a_log", "gdn_dt_bias"):
        if k in inp:
            ex[k] = np.ascontiguousarray(inp[k], dtype=np.float32)
    return ex


NT = TT // 128


def _la_alloc(self, DV=512):
    P = self.P
    B = {}
    B["qk"] = [P.tile(f"la_qk{i}", [128, 4, 2, TT], BF16) for i in range(2)]
    B["kbt"] = P.tile("la_kbt", [128, NT, 256], BF16)
    B["v"] = P.tile("la_v", [128, DV], BF16)
    B["gs"] = P.tile("la_gs", [128, DV], F32)
    B["oT"] = P.tile("la_oT", [128, 4, TT], BF16)
    B["S"] = P.tile("la_S", [128, 2, DV], F32)
    B["Se"] = P.tile("la_Se", [128, 2, DV], BF16)
    B["sT"] = P.tile("la_sT", [128, 128], BF16)
    B["t1"] = P.tile("la_t1", [128, 128], F32)
    B["oc"] = P.tile("la_oc", [128, DV], F32)
    B["junk"] = P.tile("la_junk", [128, DV], F32)
    B["y"] = P.tile("la_y", [128, DV], BF16)
    B["st"] = P.tile("la_st", [128, 8], F32)
    B["gain"] = P.tile("la_gain", [128, DV], F32)
    B["mlo"] = P.tile("la_mlo", [128, 128], F32)
    B["mup"] = P.tile("la_mup", [128, 128], F32)
    P.dma("sp", B["mlo"], self.dram["cst"][:, self.cfg["cst_mlo"]:self.cfg["cst_mlo"] + 128])
    P.dma("sp", B["mup"], self.dram["cst"][:, self.cfg["cst_mup"]:self.cfg["cst_mup"] + 128])
    return B


def _la_head(self, B, qk, E, wv, wg, vcol, gcol, state_dram, pas, center, gain_src):
    P = self.P
    DV = 512
    S, Se, sT, t1, oc, junk, y, st = B["S"], B["Se"], B["sT"], B["t1"], B["oc"], B["junk"], B["y"], B["st"]
    if pas == 0:
        P.memset("dve", S, 0.0)
    else:
        P.dma("sp", S, state_dram.re("p (k v) -> p k v", k=2))
    P.dma("sp", B["gain"], gain_src)
    for j in range(NT):
        for dk in range(2):
            P.tr(self.psb[:, dk * 128:(dk + 1) * 128], qk[:, 3, dk, j * 128:(j + 1) * 128], self.identb)
        P.copy("act", B["kbt"][:, j, :], self.psb[:, 0:256])
    for j in range(NT):
        cs = slice(j * 128, (j + 1) * 128)
        psv = self.next_ps()
        for dc in range(DC):
            P.mm(psv, self.hn[dc][:, cs], wv[:, dc, vcol:vcol + DV], start=(dc == 0), stop=(dc == DC - 1))
        P.copy("act", B["v"], psv)
        psg = self.next_ps()
        for dc in range(DC):
            P.mm(psg, self.hn[dc][:, cs], wg[:, dc, gcol:gcol + DV], start=(dc == 0), stop=(dc == DC - 1))
        P.act(B["gs"], psg, AF.Silu)
        P.tt("dve", B["gs"], B["gs"], B["gain"], ALU.mult)
        pa = self.next_ps()
        for dk in range(2):
            P.mm(pa[:, 0:128], qk[:, 3, dk, cs], qk[:, 0, dk, cs], start=(dk == 0), stop=(dk == 1))
        pb = self.next_ps()
        for dk in range(2):
            P.mm(pb[:, 0:128], qk[:, 2, dk, cs], qk[:, 1, dk, cs], start=(dk == 0), stop=(dk == 1))
        P.tt("dve", t1, pa[:, 0:128], B["mlo"], ALU.mult)
        P.tt("dve", sT, pb[:, 0:128], B["mup"], ALU.mult)
        P.tt("dve", sT, sT, t1, ALU.add)
        for dk in range(2):
            P.ts("dve", Se[:, dk, :], S[:, dk, :], E(j, dk, 0), None, ALU.mult)
        po = self.next_ps()
        P.mm(po, sT, B["v"], start=True, stop=False)
        for dk in range(2):
            P.mm(po, qk[:, 0, dk, cs], Se[:, dk, :], start=False, stop=(dk == 1))
        for dk in range(2):
            pu = self.next_ps()
            P.mm(pu, B["kbt"][:, j, dk * 128:(dk + 1) * 128], B["v"])
            P.ts("dve", S[:, dk, :], S[:, dk, :], E(j, dk, 1), None, ALU.mult)
            P.stt("dve", S[:, dk, :], pu, E(j, dk, 2), S[:, dk, :], ALU.mult, ALU.add)
        if center:
            P.I("dve", lambda e, o=_ap(st[:, 0:1]), i=_ap(po): e.reduce_sum(o, i, AX.X), reads=[po], writes=[st])
            P.ts("dve", st[:, 1:2], st[:, 0:1], 1.0 / DV, None, ALU.mult)
            P.ts("dve", oc, po, st[:, 1:2], None, ALU.subtract)
        else:
            P.copy("act", oc, po)
        P.act(junk, oc, AF.Square)
        P.I("dve", lambda e, o=_ap(st[:, 2:3]), i=_ap(junk): e.reduce_sum(o, i, AX.X), reads=[junk], writes=[st])
        P.ts("dve", st[:, 3:4], st[:, 2:3], 1.0 / DV, EPS, ALU.mult, ALU.add)
        P.act(st[:, 4:5], st[:, 3:4], AF.Sqrt)
        P.recip(st[:, 5:6], st[:, 4:5])
        P.stt("dve", y, oc, st[:, 5:6], B["gs"], ALU.mult, ALU.mult)
        for vc in range(4):
            P.tr(self.psb[:, 512 + vc * 128:512 + (vc + 1) * 128], y[:, vc * 128:(vc + 1) * 128], self.identb)
        P.copy("act", B["oT"][:, :, cs], self.psb[:, 512:1024].re("p (c t) -> p c t", c=4))
    P.dma("sp", state_dram.re("p (k v) -> p k v", k=2), S)


def _mixer_ret(self, layer, pas):
    P = self.P
    cfg = self.cfg
    d = self.dram
    P.barrier()
    self.rmsnorm("norm1", layer, self.hn)
    P.sb_off = self.scratch_base
    B = self.la_alloc()
    cs_ = P.tile("ret_cs", [128, 2, TT], F32)
    P.dma("sp", cs_[:, 0, :], d["rope"][:, pas * TT:(pas + 1) * TT])
    P.dma("sp", cs_[:, 1, :], d["rope"][:, SEQ + pas * TT:SEQ + (pas + 1) * TT])
    tab = P.tile("ret_tab", [128, 8, 4, 128], F32)
    P.dma("sp", tab, d["cst"][:, cfg["cst_rtab"]:cfg["cst_rtab"] + 8 * 4 * 128].rearrange("p (h f t) -> p h f t", h=8, f=4))
    Et = P.tile("ret_E", [128, 8, 3], F32)
    P.dma("sp", Et, d["cst"][:, cfg["cst_rE"]:cfg["cst_rE"] + 24].rearrange("p (h e) -> p h e", h=8))
    ta = P.tile("ret_ta", [128, TT], F32)
    tb = P.tile("ret_tb", [128, TT], F32)
    r1 = P.tile("ret_r1", [128, TT], F32)
    r2 = P.tile("ret_r2", [128, TT], F32)
    cosv, sinv = cs_[:, 0, :], cs_[:, 1, :]
    for hp in range(4):
        wq = self.wq.get(("ret_w_in", 0, "cols", hp * 512, 512), DC, 512)
        wk = self.wq.get(("ret_w_in", 0, "cols", 2048 + hp * 512, 512), DC, 512)
        for hh in range(2):
            h = hp * 2 + hh
            qk = B["qk"][hh]
            for which, w in ((0, wq), (1, wk)):
                p1 = self.next_ps()
                p2 = self.next_ps()
                for half, pp in ((0, p1), (1, p2)):
                    c0 = hh * 256 + half * 128
                    for dc in range(DC):
                        P.mm(pp, w[:, dc, c0:c0 + 128], self.hn[dc], start=(dc == 0), stop=(dc == DC - 1))
                P.tt("dve", ta, p1, cosv, ALU.mult)
                P.tt("dve", tb, p2, sinv, ALU.mult)
                P.tt("dve", r1, ta, tb, ALU.subtract)
                P.tt("dve", ta, p1, sinv, ALU.mult)
                P.tt("dve", tb, p2, cosv, ALU.mult)
                P.tt("dve", r2, ta, tb, ALU.add)
                for dk, r in ((0, r1), (1, r2)):
                    for fb in range(2):
                        tv = tab[:, h, which * 2 + fb, :]
                        P.tt("dve", qk[:, which * 2 + fb, dk, :].re("p (j t) -> p j t", t=128),
                             r.re("p (j t) -> p j t", t=128),
                             tv.w(_ap(tv).unsqueeze(1).to_broadcast([128, NT, 128])) if not P.dry else tv, ALU.mult)
        for hh in range(2):
            h = hp * 2 + hh
            wv = self.wq.get(("ret_w_in", 0, "cols", 4096 + h * 512, 512), DC, 512)
            wg = self.wq.get(("ret_w_in", 0, "cols", 8192 + h * 512, 512), DC, 512)
            E = lambda j, dk, which, h=h: Et[:, h, which:which + 1]
            gsrc = d["ret_gn_gain"][0, h:h + 1, :].partition_broadcast(128) if not P.dry else DUMMY
            self.la_head(B, B["qk"][hh], E, wv, wg, 0, 0, self.st_view("st_ret", h), pas, True, gsrc)
            wo = self.wq.get(("ret_w_out", 0, "rows", h * 512, 512), 4, D)
            self.out_proj_acc(wo, B["oT"], 4)


def _st_view(self, name, h):
    if self.P.dry:
        return V(DUMMY, [])
    key = (name, h)
    if key not in self._stv:
        self._stv[key] = self.P.dram_view(self.dram[name][h], f"{name}{h}")
    return self._stv[key]


Net.la_alloc = _la_alloc
Net.la_head = _la_head
Net.mixer_ret = _mixer_ret
Net.st_view = _st_view


GELU_C = 0.7978845608028654


def _mixer_lru(self, layer, pas):
    P = self.P
    P.barrier()
    self.rmsnorm("norm1", layer, self.hn)
    P.sb_off = self.scratch_base
    T = TT
    xc = P.tile("lru_xc", [128, 3 + T], F32)
    yb = P.tile("lru_yb", [128, T], F32)
    xb = P.tile("lru_xb", [128, T], F32)
    hs = P.tile("lru_hs", [128, T], F32)
    tr_ = [P.tile(f"lru_r{i}", [128, 512], F32) for i in range(2)]
    ti_ = [P.tile(f"lru_i{i}", [128, 512], F32) for i in range(2)]
    ta_ = [P.tile(f"lru_a{i}", [128, 512], F32) for i in range(2)]
    tu_ = [P.tile(f"lru_u{i}", [128, 512], F32) for i in range(2)]
    oT = P.tile("lru_oT", [128, 4, T], BF16)
    wg = P.tile("lru_wg", [128, 2, 4, 128], F32)
    cl = P.tile("lru_cl", [128, 16], F32)
    P.act(cl, self.pcol("lru_lambda", 0, 16), AF.Exp, scale=-1.0)
    P.act(cl, cl, AF.Ln, bias=1.0)
    P.ts("dve", cl, cl, -8.0, None, ALU.mult)
    hstate = self.lru_h
    hist = self.lru_hist
    if pas == 0:
        P.memset("dve", hstate, 0.0)
        P.memset("dve", hist, 0.0)
    d = self.dram
    for g in range(4):
        wx = self.wq.get(("lru_w_in", 0, "cols", g * 512, 512), DC, 512)
        wy = self.wq.get(("lru_w_in", 0, "cols", 2048 + g * 512, 512), DC, 512)
        P.dma("sp", wg[:, 0], d["lru_w_rgate"][0, g * 4:(g + 1) * 4].rearrange("n d e -> d n e"))
        P.dma("sp", wg[:, 1], d["lru_w_igate"][0, g * 4:(g + 1) * 4].rearrange("n d e -> d n e"))
        for nn in range(4):
            n = g * 4 + nn
            cs = slice(nn * 128, (nn + 1) * 128)
            P.copy("dve", xc[:, 0:3], hist[:, n, :])
            for hf in range(NH):
                tsl = slice(hf * 512, (hf + 1) * 512)
                ps = self.next_ps()
                for dc in range(DC):
                    P.mm(ps, wx[:, dc, cs], self.hn[dc][:, tsl], start=(dc == 0), stop=(dc == DC - 1))
                P.copy("act", xc[:, 3 + hf * 512:3 + (hf + 1) * 512], ps)
                ps = self.next_ps()
                for dc in range(DC):
                    P.mm(ps, wy[:, dc, cs], self.hn[dc][:, tsl], start=(dc == 0), stop=(dc == DC - 1))
                y = yb[:, tsl]
                t = tu_[hf]
                P.copy("act", y, ps)
                P.act(t, y, AF.Square)
                P.ts("dve", t, t, 0.044715, 1.0, ALU.mult, ALU.add)
                P.tt("dve", t, t, y, ALU.mult)
                P.act(t, t, AF.Sigmoid, scale=2.0 * GELU_C)
                P.tt("dve", y, t, y, ALU.mult)
            P.copy("dve", hist[:, n, :], xc[:, T:T + 3])
            P.ts("dve", xb, xc[:, 0:T], self.pcol("lru_conv_w", 0, 1, 0 * 16 + n), self.pcol("lru_conv_b", 0, 1, n),
                 ALU.mult, ALU.add)
            for tap in range(1, 4):
                P.stt("dve", xb, xc[:, tap:tap + T], self.pcol("lru_conv_w", 0, 1, tap * 16 + n), xb, ALU.mult, ALU.add)
            for hf in range(NH):
                tsl = slice(hf * 512, (hf + 1) * 512)
                r, ig, a, u = tr_[hf], ti_[hf], ta_[hf], tu_[hf]
                ps = self.next_ps()
                P.mm(ps, wg[:, 0, nn, :], xb[:, tsl])
                P.act(r, ps, AF.Sigmoid, bias=self.pcol("lru_b_rgate", 0, 1, n))
                ps = self.next_ps()
                P.mm(ps, wg[:, 1, nn, :], xb[:, tsl])
                P.act(ig, ps, AF.Sigmoid, bias=self.pcol("lru_b_igate", 0, 1, n))
                P.act(a, r, AF.Exp, scale=cl[:, n:n + 1])
                P.tt("dve", u, a, a, ALU.mult)
                P.ts("dve", u, u, -1.0, 1.0, ALU.mult, ALU.add)
                P.act(u, u, AF.Sqrt)
                P.tt("dve", ig, ig, xb[:, tsl], ALU.mult)
                P.tt("dve", u, u, ig, ALU.mult)
                ao, uo, ho, io = _ap(a), _ap(u), _ap(hs[:, tsl]), _ap(hstate[:, n:n + 1])
                P.I("dve", lambda e, ao=ao, uo=uo, ho=ho, io=io: e.tensor_tensor_scan(ho, ao, uo, io, ALU.mult, ALU.add),
                    reads=[a, u, hstate], writes=[hs])
                P.copy("dve", hstate[:, n:n + 1], hs[:, (hf + 1) * 512 - 1:(hf + 1) * 512])
            P.tt("dve", oT[:, nn, :], hs, yb, ALU.mult)
        wo = self.wq.get(("lru_w_out", 0, "rows", g * 512, 512), 4, D)
        self.out_proj_acc(wo, oT, 4)


def _out_proj_acc(self, wo, oT, nk):
    P = self.P
    for dc in range(DC):
        for hf in range(NH):
            ps = self.next_ps()
            for k in range(nk):
                P.mm(ps, wo[:, k, dc * 128:(dc + 1) * 128], oT[:, k, hf * 512:(hf + 1) * 512],
                     start=(k == 0), stop=(k == nk - 1))
            P.tt("dve", self.xres[dc][hf], self.xres[dc][hf], ps, ALU.add)


Net.mixer_lru = _mixer_lru
Net.out_proj_acc = _out_proj_acc


def _mixer_gla(self, layer, pas):
    P = self.P
    cfg = self.cfg
    d = self.dram
    P.barrier()
    self.rmsnorm("norm1", layer, self.hn)
    P.sb_off = self.scratch_base
    B = self.la_alloc()
    glT = P.tile("gla_glT", [64, TT], BF16)
    wguf = P.tile("gla_wguf", [64, 1024], F32)
    wgu = P.tile("gla_wgu", [64, 1024], BF16)
    lsp = P.tile("gla_lsp", [128, 256], F32)
    fw = P.tile("gla_fw", [128, 2, TT], BF16)
    bw = P.tile("gla_bw", [128, 2, TT], BF16)
    Et = [P.tile(f"gla_E{i}", [128, 2, NT, 3], F32) for i in range(2)]
    ltx = P.tile("gla_ltx", [128, 132], F32)
    P.dma("sp", ltx, d["cst"][:, cfg["cst_ltx"]:cfg["cst_ltx"] + 132])
    P.ts("dve", ltx, ltx, -1.0 / 16.0, None, ALU.mult)
    P.dma("sp", wguf, d["wgu"][:, :])
    P.copy("dve", wgu, wguf)
    wl = self.wq.get(("gla_w_in", 0, "cols", 6144, 16), DC, 16)
    P.memset("dve", glT, 0.0)
    P.memset("dve", glT[32:33, :], 1.0)
    ps = self.next_ps()
    for dc in range(DC):
        P.mm(ps[0:16, :], wl[:, dc, :], self.hn[dc], start=(dc == 0), stop=(dc == DC - 1))
    P.copy("act", glT[0:16, :], ps[0:16, :])
    scale = 256 ** -0.5
    for hp in range(2):
        wq = self.wq.get(("gla_w_in", 0, "cols", hp * 512, 512), DC, 512)
        wk = self.wq.get(("gla_w_in", 0, "cols", 1024 + hp * 512, 512), DC, 512)
        for hh in range(2):
            h = hp * 2 + hh
            qk = B["qk"][hh]
            for j in range(NT):
                cs = slice(j * 128, (j + 1) * 128)
                pg = self.next_ps()
                P.mm(pg[:, 0:256], glT[0:33, cs], wgu[0:33, h * 256:(h + 1) * 256])
                P.act(lsp, pg[:, 0:256], AF.Exp, scale=-1.0)
                P.act(lsp, lsp, AF.Ln, bias=1.0)
                for dk in range(2):
                    pc = self.next_ps()
                    P.mm(pc[:, 0:131], lsp[:, dk * 128:(dk + 1) * 128], ltx[:, 0:131])
                    P.act(fw[:, dk, cs], pc[:, 0:128], AF.Exp)
                    P.act(bw[:, dk, cs], pc[:, 0:128], AF.Exp, scale=-1.0)
                    P.act(Et[hh][:, dk, j, :], pc[:, 128:131], AF.Exp)
            for which, w in ((0, wq), (1, wk)):
                for dk in range(2):
                    pp = self.next_ps()
                    c0 = hh * 256 + dk * 128
                    for dc in range(DC):
                        P.mm(pp, w[:, dc, c0:c0 + 128], self.hn[dc], start=(dc == 0), stop=(dc == DC - 1))
                    if which == 0:
                        P.stt("dve", qk[:, 0, dk, :], pp, scale, fw[:, dk, :], ALU.mult, ALU.mult)
                        P.stt("dve", qk[:, 1, dk, :], pp, scale, bw[:, dk, :], ALU.mult, ALU.mult)
                    else:
                        P.tt("dve", qk[:, 2, dk, :], pp, fw[:, dk, :], ALU.mult)
                        P.tt("dve", qk[:, 3, dk, :], pp, bw[:, dk, :], ALU.mult)
        for hh in range(2):
            h = hp * 2 + hh
            wv = self.wq.get(("gla_w_in", 0, "cols", 2048 + h * 512, 512), DC, 512)
            wr = self.wq.get(("gla_w_in", 0, "cols", 4096 + h * 512, 512), DC, 512)
            E = lambda j, dk, which, e=Et[hh]: e[:, dk, j, which:which + 1]
            gsrc = d["gla_norm_gain"][0, h:h + 1, :].partition_broadcast(128) if not P.dry else DUMMY
            self.la_head(B, B["qk"][hh], E, wv, wr, 0, 0, self.st_view("st_gla", h), pas, False, gsrc)
            wo = self.wq.get(("gla_w_out", 0, "rows", h * 512, 512), 4, D)
            self.out_proj_acc(wo, B["oT"], 4)


Net.mixer_gla = _mixer_gla


def _mixer_gdn(self, layer, pas):
    P = self.P
    cfg = self.cfg
    d = self.dram
    P.barrier()
    self.rmsnorm("norm1", layer, self.hn)
    P.sb_off = self.scratch_base
    nrot_save = self.nrot
    self.nrot = 6
    paccs = [self.ps[6], self.psb.w(self.psb.ap.bitcast(F32))]

    def cmat(name):
        t = P.tile("gdn_" + name, [128, 128], F32)
        P.dma("sp", t, d["cst"][:, cfg[name]:cfg[name] + 128])
        return t

    cumT, selA, selB, strict, blk = (cmat(n) for n in ("cst_cumT", "cst_selA", "cst_selB", "cst_strict", "cst_blk"))
    ones32 = P.tile("gdn_ones", [128, 128], F32)
    P.memset("dve", ones32, 1.0)
    alog = P.tile("gdn_alog", [128, 16], F32)
    dtb = P.tile("gdn_dtb", [128, 16], F32)
    if not P.dry:
        P.dma("sp", alog, d["gdn_a_log"][0:1, :].partition_broadcast(128))
        P.dma("sp", dtb, d["gdn_dt_bias"][0:1, :].partition_broadcast(128))
    nea = P.tile("gdn_nea", [128, 16], F32)
    P.act(nea, alog, AF.Exp)
    P.ts("dve", nea, nea, -1.0, None, ALU.mult)
    sc = P.tile("gdn_sc", [128, NT, 4, 16], F32)
    eAB = P.tile("gdn_eAB", [128, NT, 2, 16], F32)
    la = P.tile("gdn_la", [128, 16], F32)
    tmp16 = P.tile("gdn_tmp16", [128, 16], F32)
    hist = self.gdn_hist
    if pas == 0:
        P.memset("dve", hist, 0.0)
    wba = self.wq.get(("gdn_w_in", 0, "cols", 8192, 32), DC, 32)
    for j in range(NT):
        cs = slice(j * 128, (j + 1) * 128)
        pb = self.next_ps()
        for dc in range(DC):
            P.mm(pb[:, 0:32], self.hn[dc][:, cs], wba[:, dc, :], start=(dc == 0), stop=(dc == DC - 1))
        P.act(sc[:, j, 0, :], pb[:, 0:16], AF.Sigmoid)
        P.tt("dve", la, pb[:, 16:32], dtb, ALU.add)
        P.act(la, la, AF.Exp)
        P.act(la, la, AF.Ln, bias=1.0)
        P.tt("dve", la, la, nea, ALU.mult)
        pc = self.next_ps()
        P.mm(pc[:, 0:16], cumT, la)
        P.mm(pc[:, 16:32], blk, la)
        P.mm(pc[:, 32:48], selA, la)
        P.mm(pc[:, 48:64], selB, la)
        P.copy("act", sc[:, j, 1, :], pc[:, 0:16])
        P.act(tmp16, pc[:, 0:16], AF.Exp)
        P.tt("dve", sc[:, j, 2, :], tmp16, sc[:, j, 0, :], ALU.mult)
        P.tt("dve", tmp16, pc[:, 16:32], sc[:, j, 1, :], ALU.subtract)
        P.act(sc[:, j, 3, :], tmp16, AF.Exp)
        P.act(eAB[:, j, :, :], pc[:, 32:64].re("p (a h) -> p a h", a=2), AF.Exp)
    xc = P.tile("gdn_xc", [128, 3 + TT], F32)
    cv = P.tile("gdn_cv", [128, TT], F32)
    sqt = P.tile("gdn_sq", [128, TT], F32)
    rs = P.tile("gdn_rs", [128, TT], F32)
    qn = P.tile("gdn_qn", [128, 4, TT], F32)
    kn = P.tile("gdn_kn", [128, 4, TT], F32)
    vn = P.tile("gdn_vn", [128, 4, TT], F32)
    sz = P.tile("gdn_sz", [128, 4, TT], F32)
    oT4 = P.tile("gdn_oT4", [128, 4, TT], BF16)
    CB = []
    for c in range(2):
        CB.append(dict(
            S=P.tile(f"gdn_S{c}", [128, 128], F32), rhs=P.tile(f"gdn_rhs{c}", [128, 256], F32),
            kend=P.tile(f"gdn_kend{c}", [128, 128], F32), diag=P.tile(f"gdn_diag{c}", [128, 128], F32),
            dn=P.tile(f"gdn_dn{c}", [128, 128], F32),
            Pk=[P.tile(f"gdn_Pk{c}_{i}", [128, 128], F32) for i in range(2)],
            PkT=[P.tile(f"gdn_PkT{c}_{i}", [128, 128], F32) for i in range(2)],
            nMT=P.tile(f"gdn_nMT{c}", [128, 128], F32)))
    gcol = self.pcol("gdn_norm_gain", 0, 1, 0)

    def conv_silu(ps, cc, out):
        P.copy("dve", xc[:, 0:3], hist[:, cc, :])
        P.copy("act", xc[:, 3:3 + TT], ps)
        P.copy("dve", hist[:, cc, :], xc[:, TT:TT + 3])
        P.ts("dve", cv, xc[:, 0:TT], self.pcol("gdn_conv_w", 0, 1, cc), None, ALU.mult)
        for tap in range(1, 4):
            P.stt("dve", cv, xc[:, tap:tap + TT], self.pcol("gdn_conv_w", 0, 1, tap * 48 + cc), cv, ALU.mult, ALU.add)
        P.act(out, cv, AF.Silu)

    for g in range(4):
        wq_ = self.wq.get(("gdn_w_in", 0, "cols", g * 512, 512), DC, 512)
        wk_ = self.wq.get(("gdn_w_in", 0, "cols", 2048 + g * 512, 512), DC, 512)
        for which, w, dst in ((0, wq_, qn), (1, wk_, kn)):
            for hh in range(4):
                h = g * 4 + hh
                ps = self.next_ps()
                for dc in range(DC):
                    P.mm(ps, w[:, dc, hh * 128:(hh + 1) * 128], self.hn[dc], start=(dc == 0), stop=(dc == DC - 1))
                conv_silu(ps, which * 16 + h, cv)
                P.act(sqt, cv, AF.Square)
                pq = self.next_ps()
                P.mm(pq, ones32, sqt)
                P.rsqrt(rs, pq, EPS)
                if which == 0:
                    P.stt("dve", dst[:, hh, :], cv, 128 ** -0.5, rs, ALU.mult, ALU.mult)
                else:
                    P.tt("dve", dst[:, hh, :], cv, rs, ALU.mult)
        wv_ = self.wq.get(("gdn_w_in", 0, "cols", 4096 + g * 512, 512), DC, 512)
        wz_ = self.wq.get(("gdn_w_in", 0, "cols", 6144 + g * 512, 512), DC, 512)
        for hh in range(4):
            h = g * 4 + hh
            ps = self.next_ps()
            for dc in range(DC):
                P.mm(ps, wv_[:, dc, hh * 128:(hh + 1) * 128], self.hn[dc], start=(dc == 0), stop=(dc == DC - 1))
            conv_silu(ps, 32 + h, vn[:, hh, :])
            ps = self.next_ps()
            for dc in range(DC):
                P.mm(ps, wz_[:, dc, hh * 128:(hh + 1) * 128], self.hn[dc], start=(dc == 0), stop=(dc == DC - 1))
            P.act(sz[:, hh, :], ps, AF.Silu)
        def head_chain(hh, c):
            h = g * 4 + hh
            cb = CB[c]
            S, rhs, kend, diag, dn, Pk, PkT, nMT = (cb[k] for k in ("S", "rhs", "kend", "diag", "dn", "Pk", "PkT", "nMT"))
            pacc = paccs[c]
            stv = self.st_view("st_gdn", h)
            if pas == 0:
                P.memset("dve", S, 0.0)
            else:
                P.dma("sp", S, stv)
            yield
            for j in range(NT):
                cs = slice(j * 128, (j + 1) * 128)
                beta = sc[:, j, 0, h:h + 1]
                cum = sc[:, j, 1, h:h + 1]
                bec = sc[:, j, 2, h:h + 1]
                eke = sc[:, j, 3, h:h + 1]
                pt = self.next_ps()
                P.tr(pt[:, 0:128], kn[:, hh, cs], self.ident)
                P.tr(pt[:, 128:256], vn[:, hh, cs], self.ident)
                P.ts("dve", diag, self.ident, cum, None, ALU.mult)
                pbm = self.next_ps()
                P.mm(pbm[:, 0:128], ones32, diag)
                pkk = self.next_ps()
                P.mm(pkk[:, 0:128], kn[:, hh, cs], kn[:, hh, cs])
                yield
                P.ts("dve", rhs[:, 0:128], pt[:, 128:256], beta, None, ALU.mult)
                P.ts("dve", rhs[:, 128:256], pt[:, 0:128], bec, None, ALU.mult)
                P.ts("dve", kend, pt[:, 0:128], eke, None, ALU.mult)
                P.ts("dve", dn, pbm[:, 0:128], cum, 0.0, ALU.subtract, ALU.max)
                P.act(dn, dn, AF.Exp, scale=-1.0)
                yield
                A, AT = Pk[0], PkT[0]
                P.tt("dve", A, pkk[:, 0:128], dn, ALU.mult)
                P.stt("dve", A, A, beta, strict, ALU.mult, ALU.mult)
                pat = self.next_ps()
                P.tr(pat[:, 0:128], A, self.ident)
                yield
                P.copy("act", AT, pat[:, 0:128])
                cur = 0
                for lvl in range(6):
                    py = self.next_ps()
                    P.mm(py[:, 0:256], PkT[cur], rhs)
                    if lvl < 5:
                        p2 = self.next_ps()
                        P.mm(p2[:, 0:128], PkT[cur], Pk[cur])
                        P.mm(p2[:, 128:256], Pk[cur], PkT[cur])
                    yield
                    P.tt("dve", rhs, rhs, py[:, 0:256], ALU.subtract if lvl == 0 else ALU.add)
                    if lvl < 5:
                        P.copy("act", Pk[1 - cur], p2[:, 0:128])
                        P.copy("act", PkT[1 - cur], p2[:, 128:256])
                        cur = 1 - cur
                for X in range(2):
                    rows = slice(X * 64, (X + 1) * 64)
                    pm = self.next_ps()
                    P.mm(pm[:, 0:128], rhs[rows, 128:256], kend[rows, :])
                    yield
                    P.ts("dve", nMT, pm[:, 0:128], -1.0, None, ALU.mult)
                    pS = self.next_ps()
                    P.mm(pS[:, 0:128], kend[rows, :], rhs[rows, 0:128], start=True, stop=False)
                    P.mm(pS[:, 0:128], nMT, S, start=False, stop=True)
                    yield
                    P.stt("dve", S, S, eAB[:, j, X, h:h + 1], pS[:, 0:128], ALU.mult, ALU.add)
                    c0 = j * 128 + X * 64
                    P.mm(pacc[:, c0:c0 + 64], S, qn[:, hh, c0:c0 + 64])
            yield
            P.dma("sp", stv, S)
            P.act(sqt, pacc, AF.Square)
            P.copy("act", cv, pacc)
            pq = self.next_ps()
            P.mm(pq, ones32, sqt)
            P.rsqrt(rs, pq, EPS, scale=1.0 / 128.0)
            P.stt("dve", cv, cv, gcol, rs, ALU.mult, ALU.mult)
            P.tt("dve", oT4[:, hh, :], cv, sz[:, hh, :], ALU.mult)

        for pair in ((0, 1), (2, 3)):
            gens = [head_chain(hh, c) for c, hh in enumerate(pair)]
            while gens:
                for gen in list(gens):
                    try:
                        next(gen)
                    except StopIteration:
                        gens.remove(gen)
        wo = self.wq.get(("gdn_w_out", 0, "rows", g * 512, 512), 4, D)
        self.out_proj_acc(wo, oT4, 4)
    self.nrot = nrot_save


Net.mixer_gdn = _mixer_gdn


ALL_WEIGHTS = ["ret_w_in", "ret_w_out", "gdn_w_in", "gdn_w_out", "gla_w_in", "gla_w_out",
               "lru_w_in", "lru_w_out", "lru_w_rgate", "lru_w_igate", "mlp_w_up", "mlp_w_down",
               "rope", "ret_gn_gain", "gla_norm_gain", "wgu", "gdn_a_log", "gdn_dt_bias"]
N_CORES = 8
NPASS = SEQ // TT


def kernel(**inputs):
    inp = {k: np.ascontiguousarray(np.asarray(v, dtype=np.float32)) for k, v in inputs.items()}
    pv, off, stride = pack_params(inp)
    cst, cidx = make_consts()
    cfg = {"npv": pv.shape[1], "pvoff": off, "pvstride": stride, "ncst": cst.shape[1]}
    cfg.update(cidx)
    layers = [(l, True, True) for l in range(4)]
    nc, P = build_program(cfg, layers, ALL_WEIGHTS, npass=NPASS)
    shared = {"cst": cst, "pv": pv}
    shared.update(extra_inputs(inp))
    for w in ALL_WEIGHTS:
        if w not in shared:
            shared[w] = inp[w]
    zeros = {k: np.zeros_like(v) for k, v in shared.items()}
    zx = np.zeros_like(inp["x"][0])
    in_maps = []
    for c in range(N_CORES):
        if c % 2 == 0:
            m = dict(shared)
            m["x"] = inp["x"][c // 2]
        else:
            m = dict(zeros)
            m["x"] = zx
        in_maps.append(m)
    res = run_bass_kernel_spmd(nc, in_maps, core_ids=list(range(N_CORES)))
    out = np.stack([np.asarray(res.results[2 * b]["y"], dtype=np.float32) for b in range(4)], axis=0)
    return out
```

```python
import numpy as np
import ml_dtypes
import concourse.bass as bass
import concourse.mybir as mybir
from concourse.bass_utils import run_bass_kernel_spmd

F32 = mybir.dt.float32
BF16 = mybir.dt.bfloat16
AF = mybir.ActivationFunctionType
ALU = mybir.AluOpType
AX = mybir.AxisListType

D = 2048
DC = D // 128
SEQ = 2048
TT = 512
NH = TT // 512
DFF = 8192
EPS = 1e-6
ERA = 30000
RING = 8
STREAMS = ("pe", "act", "dve", "pool", "sp")


class Buf:
    __slots__ = ("name", "w", "r", "psum", "keep")

    def __init__(self, name, psum=False):
        self.name = name
        self.w = None
        self.r = {}
        self.psum = psum
        self.keep = False


class V:
    __slots__ = ("ap", "bufs")

    def __init__(self, ap, bufs):
        self.ap = ap
        self.bufs = bufs

    def __getitem__(self, idx):
        return V(self.ap[idx], self.bufs)

    def re(self, s, **kw):
        return V(self.ap.rearrange(s, **kw), self.bufs)

    def bc(self, shape):
        return V(self.ap.to_broadcast(shape), self.bufs)

    def w(self, ap):
        return V(ap, self.bufs)


def _ap(x):
    return x.ap if isinstance(x, V) else x


class _Dummy:
    def __getitem__(self, idx):
        return self

    def rearrange(self, *a, **k):
        return self

    def to_broadcast(self, *a, **k):
        return self

    def bitcast(self, *a, **k):
        return self


DUMMY = _Dummy()


class Prog:
    def __init__(self, nc, dry=False):
        self.nc = nc
        self.dry = dry
        self.q = {s: [] for s in STREAMS}
        self.pend = {s: [] for s in STREAMS}
        self.seen_c = {s: {} for s in STREAMS}
        self.seen_d = {s: {} for s in STREAMS}
        self.ndma = {s: 0 for s in STREAMS}
        self.bufs = []
        self.dma_tokens = []
        self.bg_tokens = []
        self.arena_base = None
        self.sb_off = 0
        self.sb_limit = 0
        self.ntile = 0
        self.psum_stack = None

    def init_arena(self, nbytes):
        self.sb_off = 0
        self.sb_limit = nbytes
        if self.dry:
            return
        a = self.nc.alloc_sbuf_tensor("arena", [128, nbytes], mybir.dt.uint8)
        self.arena_base = self.nc.lookup_mloc(a).addr
        self.sb_off = 0
        self.sb_limit = nbytes

    def tile(self, name, shape, dtype, off=None):
        esz = 2 if dtype == BF16 else 4
        n = 1
        for s in shape[1:]:
            n *= s
        nbytes = ((n * esz + 63) // 64) * 64
        if off is None:
            off = self.sb_off
            self.sb_off += nbytes
        assert off + nbytes <= self.sb_limit, (name, off, nbytes, self.sb_limit)
        self.ntile += 1
        if self.dry:
            return V(DUMMY, [])
        t = self.nc.alloc_sbuf_tensor_at(f"{name}_{self.ntile}", list(shape), dtype,
                                         offset=self.arena_base + off)
        b = Buf(name)
        self.bufs.append(b)
        return V(t.ap(), [b])

    def psum(self, name, shape, dtype=F32):
        if self.dry:
            return V(DUMMY, [])
        t = self.nc.alloc_psum_tensor(name, list(shape), dtype)
        b = Buf(name, psum=True)
        self.bufs.append(b)
        return V(t.ap(), [b])

    def dram_view(self, ap, name="dram"):
        b = Buf(name)
        self.bufs.append(b)
        return V(ap, [b])

    def _need(self, stream, tok, same_ok):
        if tok is None:
            return
        if tok[0] == "c":
            _, e, idx = tok
            if e == stream and (same_ok or stream == "pe"):
                return
            if self.seen_c[stream].get(e, -1) >= idx:
                return
            self.seen_c[stream][e] = idx
            self.q[e][idx]["signal"] = True
            self.pend[stream].append(tok)
        else:
            _, sem, val = tok
            if self.seen_d[stream].get(sem, -1) >= val:
                return
            self.seen_d[stream][sem] = val
            self.pend[stream].append(tok)

    def I(self, stream, fn, reads=(), writes=(), dma=False, bg=False):
        if self.dry:
            return None
        rb = []
        for v in reads:
            if isinstance(v, V):
                rb.extend(v.bufs)
        wb = []
        for v in writes:
            if isinstance(v, V):
                wb.extend(v.bufs)
        for b in rb:
            self._need(stream, b.w, False)
            if b.psum:
                for t in b.r.values():
                    self._need(stream, t, True)
        for b in wb:
            self._need(stream, b.w, False)
            for t in b.r.values():
                self._need(stream, t, False)
        idx = len(self.q[stream])
        rec = {"fn": fn, "waits": self.pend[stream], "signal": False, "dma": None}
        self.pend[stream] = []
        if dma:
            n = self.ndma[stream]
            self.ndma[stream] += 1
            slot = n % RING
            val = 16 * (n // RING + 1)
            if n >= RING:
                t = ("d", (stream, slot), val - 16)
                if self.seen_d[stream].get(t[1], -1) < t[2]:
                    self.seen_d[stream][t[1]] = t[2]
                    rec["waits"].append(t)
            rec["dma"] = ((stream, slot), val)
            tok = ("d", (stream, slot), val)
            if not bg:
                self.dma_tokens.append(tok)
            else:
                self.bg_tokens.append(tok)
        else:
            tok = ("c", stream, idx)
        self.q[stream].append(rec)
        for b in rb:
            b.r[tok[:2]] = tok
        for b in wb:
            b.w = tok
            b.r = {}
        return tok

    def barrier(self, final=False):
        if self.dry:
            return
        if final:
            self.dma_tokens += self.bg_tokens
            self.bg_tokens = []
        toks = []
        for s in STREAMS:
            for idx in range(len(self.q[s]) - 1, -1, -1):
                if self.q[s][idx]["dma"] is None:
                    toks.append(("c", s, idx))
                    break
        last = {}
        for t in self.dma_tokens:
            last[t[1]] = max(last.get(t[1], 0), t[2])
        for sem, val in last.items():
            toks.append(("d", sem, val))
        for s in STREAMS:
            for t in toks:
                self._need(s, t, False)
        self.dma_tokens = []
        for b in self.bufs:
            if b.keep and not final:
                continue
            b.w = None
            b.r = {}

    def mm(self, out, lhsT, rhs, start=True, stop=True):
        o, l, r = _ap(out), _ap(lhsT), _ap(rhs)
        return self.I("pe", lambda e: e.matmul(o, l, r, start=start, stop=stop),
                      reads=[lhsT, rhs], writes=[out])

    def tr(self, out, in_, ident):
        o, i, d = _ap(out), _ap(in_), _ap(ident)
        return self.I("pe", lambda e: e.transpose(o, i, d), reads=[in_, ident], writes=[out])

    def act(self, out, in_, func, bias=0.0, scale=1.0, accum_out=None, eng="act"):
        o, i, b, s = _ap(out), _ap(in_), _ap(bias), _ap(scale)
        kw = {}
        if accum_out is not None:
            kw["accum_out"] = _ap(accum_out)
        return self.I(eng, lambda e: e.activation(o, i, func, bias=b, scale=s, **kw),
                      reads=[in_, bias, scale], writes=[out] + ([accum_out] if accum_out is not None else []))

    def tt(self, eng, out, a, b, op):
        o, x, y = _ap(out), _ap(a), _ap(b)
        return self.I(eng, lambda e: e.tensor_tensor(o, x, y, op), reads=[a, b], writes=[out])

    def ts(self, eng, out, a, s1, s2, op0, op1=None, accum_out=None):
        o, x, p, q = _ap(out), _ap(a), _ap(s1), _ap(s2)
        kw = {}
        if op1 is not None:
            kw["op1"] = op1
        if accum_out is not None:
            kw["accum_out"] = _ap(accum_out)
        return self.I(eng, lambda e: e.tensor_scalar(o, x, p, q, op0, **kw),
                      reads=[a, s1, s2], writes=[out] + ([accum_out] if accum_out is not None else []))

    def stt(self, eng, out, a, scalar, b, op0, op1):
        o, x, s, y = _ap(out), _ap(a), _ap(scalar), _ap(b)
        return self.I(eng, lambda e: e.scalar_tensor_tensor(o, x, s, y, op0, op1),
                      reads=[a, scalar, b], writes=[out])

    def copy(self, eng, out, in_):
        o, i = _ap(out), _ap(in_)
        if eng == "act":
            return self.I(eng, lambda e: e.copy(o, i), reads=[in_], writes=[out])
        return self.I(eng, lambda e: e.tensor_copy(o, i), reads=[in_], writes=[out])

    def recip(self, out, in_):
        o, i = _ap(out), _ap(in_)
        return self.I("dve", lambda e: e.reciprocal(o, i), reads=[in_], writes=[out])

    def rsqrt(self, out, in_, eps, scale=1.0):
        self.act(out, in_, AF.Sqrt, bias=eps, scale=scale)
        self.recip(out, out)

    def memset(self, eng, out, val):
        o = _ap(out)
        return self.I(eng, lambda e: e.memset(o, val), writes=[out])

    def dma(self, stream, out, in_, bg=False):
        o, i = _ap(out), _ap(in_)
        return self.I(stream, lambda e: e.dma_start(out=o, in_=i), reads=[in_], writes=[out], dma=True, bg=bg)

    def emit(self, final_tokens):
        nc = self.nc
        for t in final_tokens:
            self._need("sp", t, False)
        self.barrier(final=True)
        cum = {}
        nsig = {}
        for s in STREAMS:
            c = 0
            arr = []
            for rec in self.q[s]:
                if rec["signal"]:
                    c += 1
                arr.append(c)
            cum[s] = arr
            nsig[s] = c
        import contextlib
        with contextlib.ExitStack() as es:
            csem = {}
            for s in STREAMS:
                n_era = (nsig[s] + ERA - 1) // ERA
                csem[s] = [es.enter_context(nc.semaphore(f"c_{s}_{k}")) for k in range(max(n_era, 1))]
            dsem = {}
            for s in STREAMS:
                if self.ndma[s]:
                    for k in range(RING):
                        dsem[(s, k)] = es.enter_context(nc.semaphore(f"d_{s}_{k}"))
            block = es.enter_context(nc.Block())
            engs = {"pe": block.tensor, "act": block.scalar, "dve": block.vector,
                    "pool": block.gpsimd, "sp": block.sync}

            def run_stream(s):
                def body(e):
                    def do_waits(ws):
                        for t in ws:
                            if t[0] == "c":
                                _, te, idx = t
                                c = cum[te][idx]
                                era = (c - 1) // ERA
                                e.wait_ge(csem[te][era], c - era * ERA)
                            else:
                                e.wait_ge(dsem[t[1]], t[2])
                    for i, rec in enumerate(self.q[s]):
                        do_waits(rec["waits"])
                        ins = rec["fn"](e)
                        if rec["dma"] is not None:
                            ins.then_inc(dsem[rec["dma"][0]], 16)
                        elif rec["signal"]:
                            c = cum[s][i]
                            ins.then_inc(csem[s][(c - 1) // ERA], 1)
                    do_waits(self.pend[s])
                return body

            for s in STREAMS:
                engs[s](run_stream(s))


WSLOT = 8192
NWS = 5


class WQ:
    def __init__(self, P, plan, resolve):
        self.P = P
        self.plan = plan
        self.resolve = resolve
        self.keys = []
        self.cache = {}
        self.scratch = None
        self.nxt = 0
        self.issued = 0
        self.slots = [P.tile(f"wslot{i}", [128, WSLOT], BF16) for i in range(NWS)]
        for sl in self.slots:
            for b in sl.bufs:
                b.keep = True

    def _issue(self, j):
        key = self.plan[j]
        ap, nk, ncol = self.resolve(key)
        n = nk * ncol
        slot = self.slots[j % NWS]
        if key in self.cache:
            self.P.dma("pool", slot[:, 0:n], self.cache[key], bg=True)
            return
        dst = slot[:, 0:n].re("p (k n) -> p k n", k=nk)
        self.P.dma("pool", dst, ap, bg=True)
        if self.scratch is not None:
            ci = len(self.cache)
            cv = self.P.dram_view(self.scratch[ci // 100][ci % 100, :, 0:n], "wcache")
            cv.bufs[0].keep = True
            self.cache[key] = cv
            self.P.dma("sp", cv, slot[:, 0:n], bg=True)

    def get(self, key, nk, ncol):
        i = self.nxt
        self.nxt += 1
        if self.P.dry:
            self.keys.append(key)
            return V(DUMMY, [])
        assert self.plan[i] == key, (i, key, self.plan[i])
        while self.issued < min(i + NWS - 1, len(self.plan)):
            self._issue(self.issued)
            self.issued += 1
        slot = self.slots[i % NWS]
        return slot[:, 0:nk * ncol].re("p (k n) -> p k n", k=nk)


class Net:
    def __init__(self, P, dram, plan, cfg):
        self.P = P
        self.dram = dram
        self.cfg = cfg
        P.init_arena(204 * 1024)
        self.xres = [[P.tile(f"x{dc}_{hf}", [128, 512], F32) for hf in range(NH)] for dc in range(DC)]
        self.ident = P.tile("ident", [128, 128], F32)
        self.identb = P.tile("identb", [128, 128], BF16)
        self.onesD = P.tile("onesD", [128, 128], BF16)
        self.pv = P.tile("pv", [128, cfg["npv"]], F32)
        self.sq = [P.tile(f"sq{i}", [128, 512], BF16) for i in range(2)]
        self.rstd = [P.tile(f"rstd{i}", [128, 512], F32) for i in range(1)]
        self.lru_h = P.tile("lru_h", [128, 16], F32)
        self.lru_hist = P.tile("lru_hist", [128, 16, 3], F32)
        self.gdn_hist = P.tile("gdn_hist", [128, 48, 3], F32)
        self.hn_off = P.sb_off
        self.hn = [P.tile(f"hn{dc}", [128, TT], BF16) for dc in range(DC)]
        self.wq = WQ(P, plan, self.resolve)
        self.scratch_base = P.sb_off
        self.ps = [P.psum(f"ps{i}", [128, 512], F32) for i in range(7)]
        self.psb = P.psum("psb", [128, 1024], BF16)
        self.ips = 0
        self.nrot = 7
        self.tmpi = 0
        self._stv = {}

    def next_ps(self):
        p = self.ps[self.ips % self.nrot]
        self.ips += 1
        return p

    def resolve(self, key):
        name, layer, kind, a, b = key
        t = self.dram[name]
        if kind == "cols":
            ap = t[layer][:, a:a + b].rearrange("(k p) n -> p k n", p=128)
            return ap, ap.shape[1], b
        if kind == "rows":
            ap = t[layer][a:a + b, :].rearrange("(k p) n -> p k n", p=128)
            return ap, ap.shape[1], ap.shape[2]
        raise ValueError(kind)

    def pcol(self, name, layer=0, n=1, off=0):
        c = self.cfg["pvoff"][name] + layer * self.cfg["pvstride"].get(name, 0) + off
        return self.pv[:, c:c + n]

    def load_consts(self):
        P = self.P
        d = self.dram
        P.dma("sp", self.ident, d["cst"][:, 0:128])
        onesf = P.tile("onesf", [128, 128], F32, off=self.scratch_base + 40 * 1024)
        P.dma("sp", onesf, d["cst"][:, 128:256])
        P.copy("dve", self.onesD, onesf)
        P.copy("dve", self.identb, self.ident)
        P.dma("sp", self.pv, d["pv"][:, :])
        P.barrier()

    def load_x(self, pas):
        P = self.P
        d = self.dram
        P.barrier()
        P.sb_off = self.scratch_base
        xin = [P.tile(f"xin{i}", [128, D], F32) for i in range(2)]
        for tt in range(TT // 128):
            xt = xin[tt % 2]
            r0 = pas * TT + tt * 128
            P.dma("sp", xt, d["x"][r0:r0 + 128, :])
            for g in range(DC // 4):
                ps = self.next_ps()
                for j in range(4):
                    dc = g * 4 + j
                    P.tr(ps[:, j * 128:(j + 1) * 128], xt[:, dc * 128:(dc + 1) * 128], self.ident)
                for j in range(4):
                    dc = g * 4 + j
                    hf, o = divmod(tt * 128, 512)
                    P.copy("dve" if g % 2 else "act", self.xres[dc][hf][:, o:o + 128], ps[:, j * 128:(j + 1) * 128])

    def mixer(self, layer, pas):
        P = self.P
        kind = layer % 4
        if kind == 0:
            self.mixer_ret(layer, pas)
        elif kind == 1:
            self.mixer_gdn(layer, pas)
        elif kind == 2:
            self.mixer_gla(layer, pas)
        else:
            self.mixer_lru(layer, pas)

    def rmsnorm(self, gname, layer, out_tiles, out_f32=False):
        P = self.P
        sq, rstd = self.sq, self.rstd
        for hf in range(NH):
            ps = self.next_ps()
            for dc in range(DC):
                s = sq[dc % 2]
                P.act(s, self.xres[dc][hf], AF.Square)
                P.mm(ps, self.onesD, s, start=(dc == 0), stop=(dc == DC - 1))
            r = rstd[0]
            P.rsqrt(r, ps, EPS)
            for dc in range(DC):
                g = self.pcol(gname, layer, 1, dc)
                P.stt("dve", out_tiles[dc][:, hf * 512:(hf + 1) * 512], self.xres[dc][hf], g, r,
                      ALU.mult, ALU.mult)

    def mlp(self, layer):
        P = self.P
        P.barrier()
        self.rmsnorm("norm2", layer, self.hn)
        P.sb_off = self.scratch_base
        hT = [P.tile(f"hT{i}", [128, 4, TT], BF16) for i in range(2)]
        rl = [P.tile(f"rl{i}", [128, 512], F32) for i in range(3)]
        irl = 0
        for fg in range(DFF // 512):
            wup = self.wq.get(("mlp_w_up", layer, "cols", fg * 512, 512), DC, 512)
            wdn = self.wq.get(("mlp_w_down", layer, "rows", fg * 512, 512), 4, D)
            h = hT[fg % 2]
            for fs in range(4):
                for hf in range(NH):
                    ps = self.next_ps()
                    for dc in range(DC):
                        P.mm(ps, wup[:, dc, fs * 128:(fs + 1) * 128], self.hn[dc][:, hf * 512:(hf + 1) * 512],
                             start=(dc == 0), stop=(dc == DC - 1))
                    t = rl[irl % 3]
                    irl += 1
                    P.act(t, ps, AF.Relu)
                    P.tt("dve", h[:, fs, hf * 512:(hf + 1) * 512], t, t, ALU.mult)
            for dc in range(DC):
                for hf in range(NH):
                    ps = self.next_ps()
                    for k in range(4):
                        P.mm(ps, wdn[:, k, dc * 128:(dc + 1) * 128], h[:, k, hf * 512:(hf + 1) * 512],
                             start=(k == 0), stop=(k == 3))
                    P.tt("dve", self.xres[dc][hf], self.xres[dc][hf], ps, ALU.add)

    def finish(self, pas):
        P = self.P
        P.barrier()
        P.sb_off = self.scratch_base
        rs = [P.tile(f"frs{i}", [128, 512], F32) for i in range(NH)]
        tmp = [P.tile(f"ftmp{i}", [128, 4, 128], F32) for i in range(3)]
        ot = [P.tile(f"ot{i}", [128, D], F32) for i in range(2)]
        for hf in range(NH):
            ps = self.next_ps()
            for dc in range(DC):
                sq = self.sq[dc % 2]
                P.act(sq, self.xres[dc][hf], AF.Square)
                P.mm(ps, self.onesD, sq, start=(dc == 0), stop=(dc == DC - 1))
            P.rsqrt(rs[hf], ps, EPS)
        toks = []
        it = 0
        for tt in range(TT // 128):
            hf, o = divmod(tt * 128, 512)
            out = ot[tt % 2]
            for g in range(DC // 4):
                t = tmp[it % 3]
                it += 1
                for j in range(4):
                    dc = g * 4 + j
                    P.stt("dve", t[:, j, :], self.xres[dc][hf][:, o:o + 128], self.pcol("final_norm", 0, 1, dc),
                          rs[hf][:, o:o + 128], ALU.mult, ALU.mult)
                ps = self.next_ps()
                for j in range(4):
                    P.tr(ps[:, j * 128:(j + 1) * 128], t[:, j, :], self.ident)
                P.copy("act", out[:, g * 512:(g + 1) * 512], ps)
            r0 = pas * TT + tt * 128
            toks.append(P.dma("sp", self.dram["y"][r0:r0 + 128, :], out))
        return toks


WEIGHT_SHAPES = {
    "mlp_w_up": [4, D, DFF], "mlp_w_down": [4, DFF, D],
    "ret_w_in": [1, D, 12288], "ret_w_out": [1, 4096, D],
    "gdn_w_in": [1, D, 8224], "gdn_w_out": [1, D, D],
    "gla_w_in": [1, D, 6160], "gla_w_out": [1, D, D],
    "lru_w_in": [1, D, 4096], "lru_w_out": [1, D, D],
    "lru_w_rgate": [1, 16, 128, 128], "lru_w_igate": [1, 16, 128, 128],
    "rope": [128, 2 * SEQ], "ret_gn_gain": [1, 8, 512], "gla_norm_gain": [1, 4, 512],
    "wgu": [64, 1024], "gdn_a_log": [1, 16], "gdn_dt_bias": [1, 16],
}


def emit_net(net, layers, npass):
    net.load_consts()
    toks = []
    for pas in range(npass):
        net.load_x(pas)
        for (layer, do_mixer, do_mlp) in layers:
            if do_mixer:
                net.mixer(layer, pas)
            if do_mlp:
                net.mlp(layer)
        toks += net.finish(pas)
    return toks


def build_program(cfg, layers, weights_used, npass=2):
    Pd = Prog(None, dry=True)
    nd = Net(Pd, _DummyDict(), None, cfg)
    emit_net(nd, layers, npass)
    plan = nd.wq.keys
    nc = bass.Bass("TRN2", target_bir_lowering=False)
    dram = {}
    dram["x"] = nc.dram_tensor("x", [npass * TT, D], F32, kind="ExternalInput").ap()
    dram["cst"] = nc.dram_tensor("cst", [128, cfg["ncst"]], F32, kind="ExternalInput").ap()
    dram["pv"] = nc.dram_tensor("pv", [128, cfg["npv"]], F32, kind="ExternalInput").ap()
    for name in weights_used:
        dram[name] = nc.dram_tensor(name, WEIGHT_SHAPES[name], F32, kind="ExternalInput").ap()
    LAST_DRAM_NAMES.clear()
    LAST_DRAM_NAMES.update(list(weights_used) + ["x", "cst", "pv"])
    dram["y"] = nc.dram_tensor("y", [npass * TT, D], F32, kind="ExternalOutput").ap()
    for name, shape in (("st_ret", [8, 128, 1024]), ("st_gla", [4, 128, 1024]), ("st_gdn", [16, 128, 128])):
        dram[name] = nc.dram_tensor(name, shape, F32, kind="Internal").ap()
    P = Prog(nc)
    net = Net(P, dram, plan, cfg)
    nuniq = len(set(plan))
    if npass > 1 and nuniq:
        net.wq.scratch = [nc.dram_tensor(f"wcache{i}", [min(100, nuniq - i * 100), 128, WSLOT], BF16, kind="Internal").ap()
                          for i in range((nuniq + 99) // 100)]
    toks = emit_net(net, layers, npass)
    P.emit(toks)
    return nc, P


class _DummyDict:
    def __getitem__(self, k):
        return DUMMY


def fvec(v):
    v = np.asarray(v, dtype=np.float32)
    return np.ascontiguousarray(v.reshape(-1, 128).T)


def pack_params(inp):
    cols = []
    off = {}
    stride = {}
    pos = 0

    def add(name, arr2d, per_layer=0):
        nonlocal pos
        off[name] = pos
        stride[name] = per_layer
        cols.append(arr2d.astype(np.float32))
        pos += arr2d.shape[1]

    add("norm1", np.concatenate([fvec(inp["norm1"][l]) for l in range(4)], axis=1), 16)
    add("norm2", np.concatenate([fvec(inp["norm2"][l]) for l in range(4)], axis=1), 16)
    add("final_norm", fvec(inp["final_norm"]))
    if "gdn_conv_w" in inp:
        add("gdn_conv_w", np.concatenate([fvec(inp["gdn_conv_w"][0, t]) for t in range(4)], axis=1))
        add("gdn_norm_gain", fvec(inp["gdn_norm_gain"][0]))
    if "lru_lambda" in inp:
        add("lru_lambda", fvec(inp["lru_lambda"][0]))
        add("lru_conv_w", np.concatenate([fvec(inp["lru_conv_w"][0, t]) for t in range(4)], axis=1))
        add("lru_conv_b", fvec(inp["lru_conv_b"][0]))
        add("lru_b_rgate", fvec(inp["lru_b_rgate"][0].reshape(-1)))
        add("lru_b_igate", fvec(inp["lru_b_igate"][0].reshape(-1)))
    pv = np.ascontiguousarray(np.concatenate(cols, axis=1))
    return pv, off, stride


RET_H = 8
RET_DK = 256


def make_consts():
    cols = []
    idx = {}
    pos = 0

    def add(name, a):
        nonlocal pos
        idx[name] = pos
        cols.append(a.astype(np.float32))
        pos += a.shape[1]

    add("ident", np.eye(128, dtype=np.float32))
    add("ones", np.full((128, 128), 1.0 / D, np.float32))
    m = np.arange(128)[:, None]
    c = np.arange(128)[None, :]
    same = (m // 64) == (c // 64)
    add("cst_mlo", (m <= c).astype(np.float32))
    add("cst_mup", ((m > c) & same).astype(np.float32))
    lg = np.log1p(-np.exp2(-5.0 - np.arange(RET_H, dtype=np.float64)))
    t = np.arange(128, dtype=np.float64)
    tab = np.zeros((RET_H, 4, 128))
    for h in range(RET_H):
        f = np.exp(lg[h] * (t - 63.0))
        b = np.exp(lg[h] * (63.0 - t))
        tab[h, 0] = f * RET_DK ** -0.5
        tab[h, 1] = b * RET_DK ** -0.5
        tab[h, 2] = f
        tab[h, 3] = b
    add("cst_rtab", np.broadcast_to(tab.reshape(1, -1), (128, RET_H * 4 * 128)))
    rE = np.stack([np.exp(lg * 64.0), np.exp(lg * 128.0), np.exp(lg * 64.0)], axis=1)
    add("cst_rE", np.broadcast_to(rE.reshape(1, -1), (128, RET_H * 3)))
    ltx = np.zeros((128, 132), np.float32)
    ltx[:, 0:128] = (m <= c).astype(np.float32) - (m <= 63).astype(np.float32)
    ltx[:, 128] = (np.arange(128) <= 63)
    ltx[:, 129] = 1.0
    ltx[:, 130] = (np.arange(128) > 63)
    add("cst_ltx", ltx)
    add("cst_cumT", ((m <= c) & same).astype(np.float32))
    add("cst_selA", np.broadcast_to((np.arange(128) < 64).astype(np.float32)[:, None], (128, 128)))
    add("cst_selB", np.broadcast_to((np.arange(128) >= 64).astype(np.float32)[:, None], (128, 128)))
    add("cst_strict", ((c > m) & same).astype(np.float32).T.copy())
    add("cst_blk", same.astype(np.float32))
    return np.ascontiguousarray(np.concatenate(cols, axis=1)), idx


def make_rope():
    inv = 10000.0 ** (-np.arange(0, RET_DK, 2, dtype=np.float32) / RET_DK)
    ang = (np.arange(SEQ, dtype=np.float32)[None, :] * inv[:, None]).astype(np.float32)
    return np.ascontiguousarray(np.concatenate([np.cos(ang), np.sin(ang)], axis=1).astype(np.float32))


LAST_DRAM_NAMES = set()


def extra_inputs(inp):
    ex = {"rope": make_rope()}
    if "gla_w_gate_up" in inp:
        w = np.zeros((64, 1024), np.float32)
        w[0:16] = inp["gla_w_gate_up"][0]
        w[32] = inp["gla_gate_bias"][0]
        ex["wgu"] = w
    for k in ("ret_gn_gain", "gla_norm_gain", "gdn_a_log", "gdn_dt_bias"):
        if k in inp:
            ex[k] = np.ascontiguousarray(inp[k], dtype=np.float32)
    return ex


NT = TT // 128


def _la_alloc(self, DV=512):
    P = self.P
    B = {}
    B["qk"] = [P.tile(f"la_qk{i}", [128, 4, 2, TT], BF16) for i in range(2)]
    B["kbt"] = P.tile("la_kbt", [128, NT, 256], BF16)
    B["v"] = P.tile("la_v", [128, DV], BF16)
    B["gs"] = P.tile("la_gs", [128, DV], F32)
    B["oT"] = P.tile("la_oT", [128, 4, TT], BF16)
    B["S"] = P.tile("la_S", [128, 2, DV], F32)
    B["Se"] = P.tile("la_Se", [128, 2, DV], BF16)
    B["sT"] = P.tile("la_sT", [128, 128], BF16)
    B["t1"] = P.tile("la_t1", [128, 128], F32)
    B["oc"] = P.tile("la_oc", [128, DV], F32)
    B["junk"] = P.tile("la_junk", [128, DV], F32)
    B["y"] = P.tile("la_y", [128, DV], BF16)
    B["st"] = P.tile("la_st", [128, 8], F32)
    B["gain"] = P.tile("la_gain", [128, DV], F32)
    B["mlo"] = P.tile("la_mlo", [128, 128], F32)
    B["mup"] = P.tile("la_mup", [128, 128], F32)
    P.dma("sp", B["mlo"], self.dram["cst"][:, self.cfg["cst_mlo"]:self.cfg["cst_mlo"] + 128])
    P.dma("sp", B["mup"], self.dram["cst"][:, self.cfg["cst_mup"]:self.cfg["cst_mup"] + 128])
    return B


def _la_head(self, B, qk, E, wv, wg, vcol, gcol, state_dram, pas, center, gain_src):
    P = self.P
    DV = 512
    S, Se, sT, t1, oc, junk, y, st = B["S"], B["Se"], B["sT"], B["t1"], B["oc"], B["junk"], B["y"], B["st"]
    if pas == 0:
        P.memset("dve", S, 0.0)
    else:
        P.dma("sp", S, state_dram.re("p (k v) -> p k v", k=2))
    P.dma("sp", B["gain"], gain_src)
    for j in range(NT):
        for dk in range(2):
            P.tr(self.psb[:, dk * 128:(dk + 1) * 128], qk[:, 3, dk, j * 128:(j + 1) * 128], self.identb)
        P.copy("act", B["kbt"][:, j, :], self.psb[:, 0:256])
    for j in range(NT):
        cs = slice(j * 128, (j + 1) * 128)
        psv = self.next_ps()
        for dc in range(DC):
            P.mm(psv, self.hn[dc][:, cs], wv[:, dc, vcol:vcol + DV], start=(dc == 0), stop=(dc == DC - 1))
        P.copy("act", B["v"], psv)
        psg = self.next_ps()
        for dc in range(DC):
            P.mm(psg, self.hn[dc][:, cs], wg[:, dc, gcol:gcol + DV], start=(dc == 0), stop=(dc == DC - 1))
        P.act(B["gs"], psg, AF.Silu)
        P.tt("dve", B["gs"], B["gs"], B["gain"], ALU.mult)
        pa = self.next_ps()
        for dk in range(2):
            P.mm(pa[:, 0:128], qk[:, 3, dk, cs], qk[:, 0, dk, cs], start=(dk == 0), stop=(dk == 1))
        pb = self.next_ps()
        for dk in range(2):
            P.mm(pb[:, 0:128], qk[:, 2, dk, cs], qk[:, 1, dk, cs], start=(dk == 0), stop=(dk == 1))
        P.tt("dve", t1, pa[:, 0:128], B["mlo"], ALU.mult)
        P.tt("dve", sT, pb[:, 0:128], B["mup"], ALU.mult)
        P.tt("dve", sT, sT, t1, ALU.add)
        for dk in range(2):
            P.ts("dve", Se[:, dk, :], S[:, dk, :], E(j, dk, 0), None, ALU.mult)
        po = self.next_ps()
        P.mm(po, sT, B["v"], start=True, stop=False)
        for dk in range(2):
            P.mm(po, qk[:, 0, dk, cs], Se[:, dk, :], start=False, stop=(dk == 1))
        for dk in range(2):
            pu = self.next_ps()
            P.mm(pu, B["kbt"][:, j, dk * 128:(dk + 1) * 128], B["v"])
            P.ts("dve", S[:, dk, :], S[:, dk, :], E(j, dk, 1), None, ALU.mult)
            P.stt("dve", S[:, dk, :], pu, E(j, dk, 2), S[:, dk, :], ALU.mult, ALU.add)
        if center:
            P.I("dve", lambda e, o=_ap(st[:, 0:1]), i=_ap(po): e.reduce_sum(o, i, AX.X), reads=[po], writes=[st])
            P.ts("dve", st[:, 1:2], st[:, 0:1], 1.0 / DV, None, ALU.mult)
            P.ts("dve", oc, po, st[:, 1:2], None, ALU.subtract)
        else:
            P.copy("act", oc, po)
        P.act(junk, oc, AF.Square)
        P.I("dve", lambda e, o=_ap(st[:, 2:3]), i=_ap(junk): e.reduce_sum(o, i, AX.X), reads=[junk], writes=[st])
        P.ts("dve", st[:, 3:4], st[:, 2:3], 1.0 / DV, EPS, ALU.mult, ALU.add)
        P.act(st[:, 4:5], st[:, 3:4], AF.Sqrt)
        P.recip(st[:, 5:6], st[:, 4:5])
        P.stt("dve", y, oc, st[:, 5:6], B["gs"], ALU.mult, ALU.mult)
        for vc in range(4):
            P.tr(self.psb[:, 512 + vc * 128:512 + (vc + 1) * 128], y[:, vc * 128:(vc + 1) * 128], self.identb)
        P.copy("act", B["oT"][:, :, cs], self.psb[:, 512:1024].re("p (c t) -> p c t", c=4))
    P.dma("sp", state_dram.re("p (k v) -> p k v", k=2), S)


def _mixer_ret(self, layer, pas):
    P = self.P
    cfg = self.cfg
    d = self.dram
    P.barrier()
    self.rmsnorm("norm1", layer, self.hn)
    P.sb_off = self.scratch_base
    B = self.la_alloc()
    cs_ = P.tile("ret_cs", [128, 2, TT], F32)
    P.dma("sp", cs_[:, 0, :], d["rope"][:, pas * TT:(pas + 1) * TT])
    P.dma("sp", cs_[:, 1, :], d["rope"][:, SEQ + pas * TT:SEQ + (pas + 1) * TT])
    tab = P.tile("ret_tab", [128, 8, 4, 128], F32)
    P.dma("sp", tab, d["cst"][:, cfg["cst_rtab"]:cfg["cst_rtab"] + 8 * 4 * 128].rearrange("p (h f t) -> p h f t", h=8, f=4))
    Et = P.tile("ret_E", [128, 8, 3], F32)
    P.dma("sp", Et, d["cst"][:, cfg["cst_rE"]:cfg["cst_rE"] + 24].rearrange("p (h e) -> p h e", h=8))
    ta = P.tile("ret_ta", [128, TT], F32)
    tb = P.tile("ret_tb", [128, TT], F32)
    r1 = P.tile("ret_r1", [128, TT], F32)
    r2 = P.tile("ret_r2", [128, TT], F32)
    cosv, sinv = cs_[:, 0, :], cs_[:, 1, :]
    for hp in range(4):
        wq = self.wq.get(("ret_w_in", 0, "cols", hp * 512, 512), DC, 512)
        wk = self.wq.get(("ret_w_in", 0, "cols", 2048 + hp * 512, 512), DC, 512)
        for hh in range(2):
            h = hp * 2 + hh
            qk = B["qk"][hh]
            for which, w in ((0, wq), (1, wk)):
                p1 = self.next_ps()
                p2 = self.next_ps()
                for half, pp in ((0, p1), (1, p2)):
                    c0 = hh * 256 + half * 128
                    for dc in range(DC):
                        P.mm(pp, w[:, dc, c0:c0 + 128], self.hn[dc], start=(dc == 0), stop=(dc == DC - 1))
                P.tt("dve", ta, p1, cosv, ALU.mult)
                P.tt("dve", tb, p2, sinv, ALU.mult)
                P.tt("dve", r1, ta, tb, ALU.subtract)
                P.tt("dve", ta, p1, sinv, ALU.mult)
                P.tt("dve", tb, p2, cosv, ALU.mult)
                P.tt("dve", r2, ta, tb, ALU.add)
                for dk, r in ((0, r1), (1, r2)):
                    for fb in range(2):
                        tv = tab[:, h, which * 2 + fb, :]
                        P.tt("dve", qk[:, which * 2 + fb, dk, :].re("p (j t) -> p j t", t=128),
                             r.re("p (j t) -> p j t", t=128),
                             tv.w(_ap(tv).unsqueeze(1).to_broadcast([128, NT, 128])) if not P.dry else tv, ALU.mult)
        for hh in range(2):
            h = hp * 2 + hh
            wv = self.wq.get(("ret_w_in", 0, "cols", 4096 + h * 512, 512), DC, 512)
            wg = self.wq.get(("ret_w_in", 0, "cols", 8192 + h * 512, 512), DC, 512)
            E = lambda j, dk, which, h=h: Et[:, h, which:which + 1]
            gsrc = d["ret_gn_gain"][0, h:h + 1, :].partition_broadcast(128) if not P.dry else DUMMY
            self.la_head(B, B["qk"][hh], E, wv, wg, 0, 0, self.st_view("st_ret", h), pas, True, gsrc)
            wo = self.wq.get(("ret_w_out", 0, "rows", h * 512, 512), 4, D)
            self.out_proj_acc(wo, B["oT"], 4)


def _st_view(self, name, h):
    if self.P.dry:
        return V(DUMMY, [])
    key = (name, h)
    if key not in self._stv:
        self._stv[key] = self.P.dram_view(self.dram[name][h], f"{name}{h}")
    return self._stv[key]


Net.la_alloc = _la_alloc
Net.la_head = _la_head
Net.mixer_ret = _mixer_ret
Net.st_view = _st_view


GELU_C = 0.7978845608028654


def _mixer_lru(self, layer, pas):
    P = self.P
    P.barrier()
    self.rmsnorm("norm1", layer, self.hn)
    P.sb_off = self.scratch_base
    T = TT
    xc = P.tile("lru_xc", [128, 3 + T], F32)
    yb = P.tile("lru_yb", [128, T], F32)
    xb = P.tile("lru_xb", [128, T], F32)
    hs = P.tile("lru_hs", [128, T], F32)
    tr_ = [P.tile(f"lru_r{i}", [128, 512], F32) for i in range(2)]
    ti_ = [P.tile(f"lru_i{i}", [128, 512], F32) for i in range(2)]
    ta_ = [P.tile(f"lru_a{i}", [128, 512], F32) for i in range(2)]
    tu_ = [P.tile(f"lru_u{i}", [128, 512], F32) for i in range(2)]
    oT = P.tile("lru_oT", [128, 4, T], BF16)
    wg = P.tile("lru_wg", [128, 2, 4, 128], F32)
    cl = P.tile("lru_cl", [128, 16], F32)
    P.act(cl, self.pcol("lru_lambda", 0, 16), AF.Exp, scale=-1.0)
    P.act(cl, cl, AF.Ln, bias=1.0)
    P.ts("dve", cl, cl, -8.0, None, ALU.mult)
    hstate = self.lru_h
    hist = self.lru_hist
    if pas == 0:
        P.memset("dve", hstate, 0.0)
        P.memset("dve", hist, 0.0)
    d = self.dram
    for g in range(4):
        wx = self.wq.get(("lru_w_in", 0, "cols", g * 512, 512), DC, 512)
        wy = self.wq.get(("lru_w_in", 0, "cols", 2048 + g * 512, 512), DC, 512)
        P.dma("sp", wg[:, 0], d["lru_w_rgate"][0, g * 4:(g + 1) * 4].rearrange("n d e -> d n e"))
        P.dma("sp", wg[:, 1], d["lru_w_igate"][0, g * 4:(g + 1) * 4].rearrange("n d e -> d n e"))
        for nn in range(4):
            n = g * 4 + nn
            cs = slice(nn * 128, (nn + 1) * 128)
            P.copy("dve", xc[:, 0:3], hist[:, n, :])
            for hf in range(NH):
                tsl = slice(hf * 512, (hf + 1) * 512)
                ps = self.next_ps()
                for dc in range(DC):
                    P.mm(ps, wx[:, dc, cs], self.hn[dc][:, tsl], start=(dc == 0), stop=(dc == DC - 1))
                P.copy("act", xc[:, 3 + hf * 512:3 + (hf + 1) * 512], ps)
                ps = self.next_ps()
                for dc in range(DC):
                    P.mm(ps, wy[:, dc, cs], self.hn[dc][:, tsl], start=(dc == 0), stop=(dc == DC - 1))
                y = yb[:, tsl]
                t = tu_[hf]
                P.copy("act", y, ps)
                P.act(t, y, AF.Square)
                P.ts("dve", t, t, 0.044715, 1.0, ALU.mult, ALU.add)
                P.tt("dve", t, t, y, ALU.mult)
                P.act(t, t, AF.Sigmoid, scale=2.0 * GELU_C)
                P.tt("dve", y, t, y, ALU.mult)
            P.copy("dve", hist[:, n, :], xc[:, T:T + 3])
            P.ts("dve", xb, xc[:, 0:T], self.pcol("lru_conv_w", 0, 1, 0 * 16 + n), self.pcol("lru_conv_b", 0, 1, n),
                 ALU.mult, ALU.add)
            for tap in range(1, 4):
                P.stt("dve", xb, xc[:, tap:tap + T], self.pcol("lru_conv_w", 0, 1, tap * 16 + n), xb, ALU.mult, ALU.add)
            for hf in range(NH):
                tsl = slice(hf * 512, (hf + 1) * 512)
                r, ig, a, u = tr_[hf], ti_[hf], ta_[hf], tu_[hf]
                ps = self.next_ps()
                P.mm(ps, wg[:, 0, nn, :], xb[:, tsl])
                P.act(r, ps, AF.Sigmoid, bias=self.pcol("lru_b_rgate", 0, 1, n))
                ps = self.next_ps()
                P.mm(ps, wg[:, 1, nn, :], xb[:, tsl])
                P.act(ig, ps, AF.Sigmoid, bias=self.pcol("lru_b_igate", 0, 1, n))
                P.act(a, r, AF.Exp, scale=cl[:, n:n + 1])
                P.tt("dve", u, a, a, ALU.mult)
                P.ts("dve", u, u, -1.0, 1.0, ALU.mult, ALU.add)
                P.act(u, u, AF.Sqrt)
                P.tt("dve", ig, ig, xb[:, tsl], ALU.mult)
                P.tt("dve", u, u, ig, ALU.mult)
                ao, uo, ho, io = _ap(a), _ap(u), _ap(hs[:, tsl]), _ap(hstate[:, n:n + 1])
                P.I("dve", lambda e, ao=ao, uo=uo, ho=ho, io=io: e.tensor_tensor_scan(ho, ao, uo, io, ALU.mult, ALU.add),
                    reads=[a, u, hstate], writes=[hs])
                P.copy("dve", hstate[:, n:n + 1], hs[:, (hf + 1) * 512 - 1:(hf + 1) * 512])
            P.tt("dve", oT[:, nn, :], hs, yb, ALU.mult)
        wo = self.wq.get(("lru_w_out", 0, "rows", g * 512, 512), 4, D)
        self.out_proj_acc(wo, oT, 4)


def _out_proj_acc(self, wo, oT, nk):
    P = self.P
    for dc in range(DC):
        for hf in range(NH):
            ps = self.next_ps()
            for k in range(nk):
                P.mm(ps, wo[:, k, dc * 128:(dc + 1) * 128], oT[:, k, hf * 512:(hf + 1) * 512],
                     start=(k == 0), stop=(k == nk - 1))
            P.tt("dve", self.xres[dc][hf], self.xres[dc][hf], ps, ALU.add)


Net.mixer_lru = _mixer_lru
Net.out_proj_acc = _out_proj_acc


def _mixer_gla(self, layer, pas):
    P = self.P
    cfg = self.cfg
    d = self.dram
    P.barrier()
    self.rmsnorm("norm1", layer, self.hn)
    P.sb_off = self.scratch_base
    B = self.la_alloc()
    glT = P.tile("gla_glT", [64, TT], BF16)
    wguf = P.tile("gla_wguf", [64, 1024], F32)
    wgu = P.tile("gla_wgu", [64, 1024], BF16)
    lsp = P.tile("gla_lsp", [128, 256], F32)
    fw = P.tile("gla_fw", [128, 2, TT], BF16)
    bw = P.tile("gla_bw", [128, 2, TT], BF16)
    Et = [P.tile(f"gla_E{i}", [128, 2, NT, 3], F32) for i in range(2)]
    ltx = P.tile("gla_ltx", [128, 132], F32)
    P.dma("sp", ltx, d["cst"][:, cfg["cst_ltx"]:cfg["cst_ltx"] + 132])
    P.ts("dve", ltx, ltx, -1.0 / 16.0, None, ALU.mult)
    P.dma("sp", wguf, d["wgu"][:, :])
    P.copy("dve", wgu, wguf)
    wl = self.wq.get(("gla_w_in", 0, "cols", 6144, 16), DC, 16)
    P.memset("dve", glT, 0.0)
    P.memset("dve", glT[32:33, :], 1.0)
    ps = self.next_ps()
    for dc in range(DC):
        P.mm(ps[0:16, :], wl[:, dc, :], self.hn[dc], start=(dc == 0), stop=(dc == DC - 1))
    P.copy("act", glT[0:16, :], ps[0:16, :])
    scale = 256 ** -0.5
    for hp in range(2):
        wq = self.wq.get(("gla_w_in", 0, "cols", hp * 512, 512), DC, 512)
        wk = self.wq.get(("gla_w_in", 0, "cols", 1024 + hp * 512, 512), DC, 512)
        for hh in range(2):
            h = hp * 2 + hh
            qk = B["qk"][hh]
            for j in range(NT):
                cs = slice(j * 128, (j + 1) * 128)
                pg = self.next_ps()
                P.mm(pg[:, 0:256], glT[0:33, cs], wgu[0:33, h * 256:(h + 1) * 256])
                P.act(lsp, pg[:, 0:256], AF.Exp, scale=-1.0)
                P.act(lsp, lsp, AF.Ln, bias=1.0)
                for dk in range(2):
                    pc = self.next_ps()
                    P.mm(pc[:, 0:131], lsp[:, dk * 128:(dk + 1) * 128], ltx[:, 0:131])
                    P.act(fw[:, dk, cs], pc[:, 0:128], AF.Exp)
                    P.act(bw[:, dk, cs], pc[:, 0:128], AF.Exp, scale=-1.0)
                    P.act(Et[hh][:, dk, j, :], pc[:, 128:131], AF.Exp)
            for which, w in ((0, wq), (1, wk)):
                for dk in range(2):
                    pp = self.next_ps()
                    c0 = hh * 256 + dk * 128
                    for dc in range(DC):
                        P.mm(pp, w[:, dc, c0:c0 + 128], self.hn[dc], start=(dc == 0), stop=(dc == DC - 1))
                    if which == 0:
                        P.stt("dve", qk[:, 0, dk, :], pp, scale, fw[:, dk, :], ALU.mult, ALU.mult)
                        P.stt("dve", qk[:, 1, dk, :], pp, scale, bw[:, dk, :], ALU.mult, ALU.mult)
                    else:
                        P.tt("dve", qk[:, 2, dk, :], pp, fw[:, dk, :], ALU.mult)
                        P.tt("dve", qk[:, 3, dk, :], pp, bw[:, dk, :], ALU.mult)
        for hh in range(2):
            h = hp * 2 + hh
            wv = self.wq.get(("gla_w_in", 0, "cols", 2048 + h * 512, 512), DC, 512)
            wr = self.wq.get(("gla_w_in", 0, "cols", 4096 + h * 512, 512), DC, 512)
            E = lambda j, dk, which, e=Et[hh]: e[:, dk, j, which:which + 1]
            gsrc = d["gla_norm_gain"][0, h:h + 1, :].partition_broadcast(128) if not P.dry else DUMMY
            self.la_head(B, B["qk"][hh], E, wv, wr, 0, 0, self.st_view("st_gla", h), pas, False, gsrc)
            wo = self.wq.get(("gla_w_out", 0, "rows", h * 512, 512), 4, D)
            self.out_proj_acc(wo, B["oT"], 4)


Net.mixer_gla = _mixer_gla


def _mixer_gdn(self, layer, pas):
    P = self.P
    cfg = self.cfg
    d = self.dram
    P.barrier()
    self.rmsnorm("norm1", layer, self.hn)
    P.sb_off = self.scratch_base
    nrot_save = self.nrot
    self.nrot = 6
    paccs = [self.ps[6], self.psb.w(self.psb.ap.bitcast(F32))]

    def cmat(name):
        t = P.tile("gdn_" + name, [128, 128], F32)
        P.dma("sp", t, d["cst"][:, cfg[name]:cfg[name] + 128])
        return t

    cumT, selA, selB, strict, blk = (cmat(n) for n in ("cst_cumT", "cst_selA", "cst_selB", "cst_strict", "cst_blk"))
    ones32 = P.tile("gdn_ones", [128, 128], F32)
    P.memset("dve", ones32, 1.0)
    alog = P.tile("gdn_alog", [128, 16], F32)
    dtb = P.tile("gdn_dtb", [128, 16], F32)
    if not P.dry:
        P.dma("sp", alog, d["gdn_a_log"][0:1, :].partition_broadcast(128))
        P.dma("sp", dtb, d["gdn_dt_bias"][0:1, :].partition_broadcast(128))
    nea = P.tile("gdn_nea", [128, 16], F32)
    P.act(nea, alog, AF.Exp)
    P.ts("dve", nea, nea, -1.0, None, ALU.mult)
    sc = P.tile("gdn_sc", [128, NT, 4, 16], F32)
    eAB = P.tile("gdn_eAB", [128, NT, 2, 16], F32)
    la = P.tile("gdn_la", [128, 16], F32)
    tmp16 = P.tile("gdn_tmp16", [128, 16], F32)
    hist = self.gdn_hist
    if pas == 0:
        P.memset("dve", hist, 0.0)
    wba = self.wq.get(("gdn_w_in", 0, "cols", 8192, 32), DC, 32)
    for j in range(NT):
        cs = slice(j * 128, (j + 1) * 128)
        pb = self.next_ps()
        for dc in range(DC):
            P.mm(pb[:, 0:32], self.hn[dc][:, cs], wba[:, dc, :], start=(dc == 0), stop=(dc == DC - 1))
        P.act(sc[:, j, 0, :], pb[:, 0:16], AF.Sigmoid)
        P.tt("dve", la, pb[:, 16:32], dtb, ALU.add)
        P.act(la, la, AF.Exp)
        P.act(la, la, AF.Ln, bias=1.0)
        P.tt("dve", la, la, nea, ALU.mult)
        pc = self.next_ps()
        P.mm(pc[:, 0:16], cumT, la)
        P.mm(pc[:, 16:32], blk, la)
        P.mm(pc[:, 32:48], selA, la)
        P.mm(pc[:, 48:64], selB, la)
        P.copy("act", sc[:, j, 1, :], pc[:, 0:16])
        P.act(tmp16, pc[:, 0:16], AF.Exp)
        P.tt("dve", sc[:, j, 2, :], tmp16, sc[:, j, 0, :], ALU.mult)
        P.tt("dve", tmp16, pc[:, 16:32], sc[:, j, 1, :], ALU.subtract)
        P.act(sc[:, j, 3, :], tmp16, AF.Exp)
        P.act(eAB[:, j, :, :], pc[:, 32:64].re("p (a h) -> p a h", a=2), AF.Exp)
    xc = P.tile("gdn_xc", [128, 3 + TT], F32)
    cv = P.tile("gdn_cv", [128, TT], F32)
    sqt = P.tile("gdn_sq", [128, TT], F32)
    rs = P.tile("gdn_rs", [128, TT], F32)
    qn = P.tile("gdn_qn", [128, 4, TT], F32)
    kn = P.tile("gdn_kn", [128, 4, TT], F32)
    vn = P.tile("gdn_vn", [128, 4, TT], F32)
    sz = P.tile("gdn_sz", [128, 4, TT], F32)
    oT4 = P.tile("gdn_oT4", [128, 4, TT], BF16)
    CB = []
    for c in range(2):
        CB.append(dict(
            S=P.tile(f"gdn_S{c}", [128, 128], F32), rhs=P.tile(f"gdn_rhs{c}", [128, 256], F32),
            kend=P.tile(f"gdn_kend{c}", [128, 128], F32), diag=P.tile(f"gdn_diag{c}", [128, 128], F32),
            dn=P.tile(f"gdn_dn{c}", [128, 128], F32),
            Pk=[P.tile(f"gdn_Pk{c}_{i}", [128, 128], F32) for i in range(2)],
            PkT=[P.tile(f"gdn_PkT{c}_{i}", [128, 128], F32) for i in range(2)],
            nMT=P.tile(f"gdn_nMT{c}", [128, 128], F32)))
    gcol = self.pcol("gdn_norm_gain", 0, 1, 0)

    def conv_silu(ps, cc, out):
        P.copy("dve", xc[:, 0:3], hist[:, cc, :])
        P.copy("act", xc[:, 3:3 + TT], ps)
        P.copy("dve", hist[:, cc, :], xc[:, TT:TT + 3])
        P.ts("dve", cv, xc[:, 0:TT], self.pcol("gdn_conv_w", 0, 1, cc), None, ALU.mult)
        for tap in range(1, 4):
            P.stt("dve", cv, xc[:, tap:tap + TT], self.pcol("gdn_conv_w", 0, 1, tap * 48 + cc), cv, ALU.mult, ALU.add)
        P.act(out, cv, AF.Silu)

    for g in range(4):
        wq_ = self.wq.get(("gdn_w_in", 0, "cols", g * 512, 512), DC, 512)
        wk_ = self.wq.get(("gdn_w_in", 0, "cols", 2048 + g * 512, 512), DC, 512)
        for which, w, dst in ((0, wq_, qn), (1, wk_, kn)):
            for hh in range(4):
                h = g * 4 + hh
                ps = self.next_ps()
                for dc in range(DC):
                    P.mm(ps, w[:, dc, hh * 128:(hh + 1) * 128], self.hn[dc], start=(dc == 0), stop=(dc == DC - 1))
                conv_silu(ps, which * 16 + h, cv)
                P.act(sqt, cv, AF.Square)
                pq = self.next_ps()
                P.mm(pq, ones32, sqt)
                P.rsqrt(rs, pq, EPS)
                if which == 0:
                    P.stt("dve", dst[:, hh, :], cv, 128 ** -0.5, rs, ALU.mult, ALU.mult)
                else:
                    P.tt("dve", dst[:, hh, :], cv, rs, ALU.mult)
        wv_ = self.wq.get(("gdn_w_in", 0, "cols", 4096 + g * 512, 512), DC, 512)
        wz_ = self.wq.get(("gdn_w_in", 0, "cols", 6144 + g * 512, 512), DC, 512)
        for hh in range(4):
            h = g * 4 + hh
            ps = self.next_ps()
            for dc in range(DC):
                P.mm(ps, wv_[:, dc, hh * 128:(hh + 1) * 128], self.hn[dc], start=(dc == 0), stop=(dc == DC - 1))
            conv_silu(ps, 32 + h, vn[:, hh, :])
            ps = self.next_ps()
            for dc in range(DC):
                P.mm(ps, wz_[:, dc, hh * 128:(hh + 1) * 128], self.hn[dc], start=(dc == 0), stop=(dc == DC - 1))
            P.act(sz[:, hh, :], ps, AF.Silu)
        def head_chain(hh, c):
            h = g * 4 + hh
            cb = CB[c]
            S, rhs, kend, diag, dn, Pk, PkT, nMT = (cb[k] for k in ("S", "rhs", "kend", "diag", "dn", "Pk", "PkT", "nMT"))
            pacc = paccs[c]
            stv = self.st_view("st_gdn", h)
            if pas == 0:
                P.memset("dve", S, 0.0)
            else:
                P.dma("sp", S, stv)
            yield
            for j in range(NT):
                cs = slice(j * 128, (j + 1) * 128)
                beta = sc[:, j, 0, h:h + 1]
                cum = sc[:, j, 1, h:h + 1]
                bec = sc[:, j, 2, h:h + 1]
                eke = sc[:, j, 3, h:h + 1]
                pt = self.next_ps()
                P.tr(pt[:, 0:128], kn[:, hh, cs], self.ident)
                P.tr(pt[:, 128:256], vn[:, hh, cs], self.ident)
                P.ts("dve", diag, self.ident, cum, None, ALU.mult)
                pbm = self.next_ps()
                P.mm(pbm[:, 0:128], ones32, diag)
                pkk = self.next_ps()
                P.mm(pkk[:, 0:128], kn[:, hh, cs], kn[:, hh, cs])
                yield
                P.ts("dve", rhs[:, 0:128], pt[:, 128:256], beta, None, ALU.mult)
                P.ts("dve", rhs[:, 128:256], pt[:, 0:128], bec, None, ALU.mult)
                P.ts("dve", kend, pt[:, 0:128], eke, None, ALU.mult)
                P.ts("dve", dn, pbm[:, 0:128], cum, 0.0, ALU.subtract, ALU.max)
                P.act(dn, dn, AF.Exp, scale=-1.0)
                yield
                A, AT = Pk[0], PkT[0]
                P.tt("dve", A, pkk[:, 0:128], dn, ALU.mult)
                P.stt("dve", A, A, beta, strict, ALU.mult, ALU.mult)
                pat = self.next_ps()
                P.tr(pat[:, 0:128], A, self.ident)
                yield
                P.copy("act", AT, pat[:, 0:128])
                cur = 0
                for lvl in range(6):
                    py = self.next_ps()
                    P.mm(py[:, 0:256], PkT[cur], rhs)
                    if lvl < 5:
                        p2 = self.next_ps()
                        P.mm(p2[:, 0:128], PkT[cur], Pk[cur])
                        P.mm(p2[:, 128:256], Pk[cur], PkT[cur])
                    yield
                    P.tt("dve", rhs, rhs, py[:, 0:256], ALU.subtract if lvl == 0 else ALU.add)
                    if lvl < 5:
                        P.copy("act", Pk[1 - cur], p2[:, 0:128])
                        P.copy("act", PkT[1 - cur], p2[:, 128:256])
                        cur = 1 - cur
                for X in range(2):
                    rows = slice(X * 64, (X + 1) * 64)
                    pm = self.next_ps()
                    P.mm(pm[:, 0:128], rhs[rows, 128:256], kend[rows, :])
                    yield
                    P.ts("dve", nMT, pm[:, 0:128], -1.0, None, ALU.mult)
                    pS = self.next_ps()
                    P.mm(pS[:, 0:128], kend[rows, :], rhs[rows, 0:128], start=True, stop=False)
                    P.mm(pS[:, 0:128], nMT, S, start=False, stop=True)
                    yield
                    P.stt("dve", S, S, eAB[:, j, X, h:h + 1], pS[:, 0:128], ALU.mult, ALU.add)
                    c0 = j * 128 + X * 64
                    P.mm(pacc[:, c0:c0 + 64], S, qn[:, hh, c0:c0 + 64])
            yield
            P.dma("sp", stv, S)
            P.act(sqt, pacc, AF.Square)
            P.copy("act", cv, pacc)
            pq = self.next_ps()
            P.mm(pq, ones32, sqt)
            P.rsqrt(rs, pq, EPS, scale=1.0 / 128.0)
            P.stt("dve", cv, cv, gcol, rs, ALU.mult, ALU.mult)
            P.tt("dve", oT4[:, hh, :], cv, sz[:, hh, :], ALU.mult)

        for pair in ((0, 1), (2, 3)):
            gens = [head_chain(hh, c) for c, hh in enumerate(pair)]
            while gens:
                for gen in list(gens):
                    try:
                        next(gen)
                    except StopIteration:
                        gens.remove(gen)
        wo = self.wq.get(("gdn_w_out", 0, "rows", g * 512, 512), 4, D)
        self.out_proj_acc(wo, oT4, 4)
    self.nrot = nrot_save


Net.mixer_gdn = _mixer_gdn


ALL_WEIGHTS = ["ret_w_in", "ret_w_out", "gdn_w_in", "gdn_w_out", "gla_w_in", "gla_w_out",
               "lru_w_in", "lru_w_out", "lru_w_rgate", "lru_w_igate", "mlp_w_up", "mlp_w_down",
               "rope", "ret_gn_gain", "gla_norm_gain", "wgu", "gdn_a_log", "gdn_dt_bias"]
N_CORES = 8
NPASS = SEQ // TT


def kernel(**inputs):
    inp = {k: np.ascontiguousarray(np.asarray(v, dtype=np.float32)) for k, v in inputs.items()}
    pv, off, stride = pack_params(inp)
    cst, cidx = make_consts()
    cfg = {"npv": pv.shape[1], "pvoff": off, "pvstride": stride, "ncst": cst.shape[1]}
    cfg.update(cidx)
    layers = [(l, True, True) for l in range(4)]
    nc, P = build_program(cfg, layers, ALL_WEIGHTS, npass=NPASS)
    shared = {"cst": cst, "pv": pv}
    shared.update(extra_inputs(inp))
    for w in ALL_WEIGHTS:
        if w not in shared:
            shared[w] = inp[w]
    zeros = {k: np.zeros_like(v) for k, v in shared.items()}
    zx = np.zeros_like(inp["x"][0])
    in_maps = []
    for c in range(N_CORES):
        if c % 2 == 0:
            m = dict(shared)
            m["x"] = inp["x"][c // 2]
        else:
            m = dict(zeros)
            m["x"] = zx
        in_maps.append(m)
    res = run_bass_kernel_spmd(nc, in_maps, core_ids=list(range(N_CORES)))
    out = np.stack([np.asarray(res.results[2 * b]["y"], dtype=np.float32) for b in range(4)], axis=0)
    return out
```
